# Optimizing a Trainium2 kernel written in Bass

```python
import jax
import jax.numpy as jnp
from jax import lax
import numpy as np

D_MODEL = 1024
BATCH = 8
SEQ = 4096
DEPTH = 1

HEAD_DIM = 64
N_SB_HEADS = 8
N_FOX_HEADS = 8
SB_WIDTH = N_SB_HEADS * HEAD_DIM
FOX_WIDTH = N_FOX_HEADS * HEAD_DIM
MIX_WIDTH = SB_WIDTH + FOX_WIDTH
IN_WIDTH = 3 * SB_WIDTH + 3 * FOX_WIDTH + N_FOX_HEADS
D_FF = 2816
BLOCK_Q = 128
N_MOD = 9
EPS = 1e-6

kernel_name = 'hymba_stickbreak_fox_macaron_adaln'


def rms_norm(x, gain):
    xf = x.astype(jnp.float32)
    y = xf * lax.rsqrt(jnp.mean(xf * xf, axis=-1, keepdims=True) + EPS)
    return (y * gain.astype(jnp.float32)).astype(x.dtype)


def modulate(h, shift, scale):
    return h * (1 + scale[:, None, :]) + shift[:, None, :]


def swiglu(h, w_gate, w_up, w_down):
    return (jax.nn.silu(h @ w_gate) * (h @ w_up)) @ w_down


def split_heads(t, n_heads):
    b, s, _ = t.shape
    return t.reshape(b, s, n_heads, HEAD_DIM).transpose(0, 2, 1, 3)


def stick_breaking_attention(q, k, v):
    seq = q.shape[2]
    scale = HEAD_DIM ** -0.5
    outs = []
    for start in range(0, seq, BLOCK_Q):
        end = start + BLOCK_Q
        z = jnp.einsum('bhqd,bhkd->bhqk', q[:, :, start:end], k[:, :, :end]).astype(jnp.float32) * scale
        mask = jnp.arange(end)[None, :] < jnp.arange(start, end)[:, None]
        log_beta = jax.nn.log_sigmoid(z)
        log_keep = jnp.where(mask, jax.nn.log_sigmoid(-z), 0.0)
        later = lax.cumsum(log_keep, axis=3, reverse=True) - log_keep
        w = jnp.where(mask, jnp.exp(log_beta + later), 0.0)
        outs.append(jnp.einsum('bhqk,bhkd->bhqd', w.astype(v.dtype), v[:, :, :end]))
    return jnp.concatenate(outs, axis=2)


def forgetting_attention(q, k, v, log_f_cum):
    seq = q.shape[2]
    scale = HEAD_DIM ** -0.5
    outs = []
    for start in range(0, seq, BLOCK_Q):
        end = start + BLOCK_Q
        z = jnp.einsum('bhqd,bhkd->bhqk', q[:, :, start:end], k[:, :, :end]).astype(jnp.float32) * scale
        z = z + log_f_cum[:, :, start:end, None] - log_f_cum[:, :, None, :end]
        mask = jnp.arange(end)[None, :] <= jnp.arange(start, end)[:, None]
        p = jax.nn.softmax(jnp.where(mask, z, -jnp.inf), axis=-1)
        outs.append(jnp.einsum('bhqk,bhkd->bhqd', p.astype(v.dtype), v[:, :, :end]))
    return jnp.concatenate(outs, axis=2)


def hybrid_mixer(h, w_in, b_f, g_q, g_k, w_o):
    proj = h @ w_in
    splits = [SB_WIDTH, 2 * SB_WIDTH, 3 * SB_WIDTH,
              3 * SB_WIDTH + FOX_WIDTH, 3 * SB_WIDTH + 2 * FOX_WIDTH, 3 * SB_WIDTH + 3 * FOX_WIDTH]
    sb_q, sb_k, sb_v, fox_q, fox_k, fox_v, fox_f = jnp.split(proj, splits, axis=-1)
    sb_out = stick_breaking_attention(split_heads(sb_q, N_SB_HEADS), split_heads(sb_k, N_SB_HEADS),
                                      split_heads(sb_v, N_SB_HEADS))
    fq = rms_norm(split_heads(fox_q, N_FOX_HEADS), g_q[None, :, None, :])
    fk = rms_norm(split_heads(fox_k, N_FOX_HEADS), g_k[None, :, None, :])
    log_f = jax.nn.log_sigmoid((fox_f + b_f).astype(jnp.float32))
    log_f_cum = jnp.cumsum(log_f, axis=1).transpose(0, 2, 1)
    fox_out = forgetting_attention(fq, fk, split_heads(fox_v, N_FOX_HEADS), log_f_cum)
    heads = jnp.concatenate([sb_out, fox_out], axis=1)
    b, _, s, _ = heads.shape
    return heads.transpose(0, 2, 1, 3).reshape(b, s, MIX_WIDTH) @ w_o


def setup_inputs(seed: int = 0) -> dict:
    key = jax.random.key(seed)
    ks = jax.random.split(key, 20)
    nrm = jax.random.normal
    d_s = D_MODEL ** -0.5
    return {
        'x': nrm(ks[0], (BATCH, SEQ, D_MODEL), jnp.float32),
        'c': nrm(ks[1], (BATCH, D_MODEL), jnp.float32),
        'w_mod': nrm(ks[2], (DEPTH, D_MODEL, N_MOD * D_MODEL), jnp.float32) * (0.1 * d_s),
        'b_mod': nrm(ks[3], (DEPTH, N_MOD * D_MODEL), jnp.float32) * 0.01,
        'g_ffn1': 1.0 + 0.02 * nrm(ks[4], (DEPTH, D_MODEL), jnp.float32),
        'w1_gate': nrm(ks[5], (DEPTH, D_MODEL, D_FF), jnp.float32) * d_s,
        'w1_up': nrm(ks[6], (DEPTH, D_MODEL, D_FF), jnp.float32) * d_s,
        'w1_down': nrm(ks[7], (DEPTH, D_FF, D_MODEL), jnp.float32) * D_FF ** -0.5,
        'g_mix': 1.0 + 0.02 * nrm(ks[8], (DEPTH, D_MODEL), jnp.float32),
        'w_in': nrm(ks[9], (DEPTH, D_MODEL, IN_WIDTH), jnp.float32) * d_s,
        'b_f': jax.random.uniform(ks[10], (DEPTH, N_FOX_HEADS), jnp.float32, 1.0, 4.0),
        'g_q': 1.0 + 0.02 * nrm(ks[11], (DEPTH, N_FOX_HEADS, HEAD_DIM), jnp.float32),
        'g_k': 1.0 + 0.02 * nrm(ks[12], (DEPTH, N_FOX_HEADS, HEAD_DIM), jnp.float32),
        'w_o': nrm(ks[13], (DEPTH, MIX_WIDTH, D_MODEL), jnp.float32) * MIX_WIDTH ** -0.5,
        'g_ffn2': 1.0 + 0.02 * nrm(ks[14], (DEPTH, D_MODEL), jnp.float32),
        'w2_gate': nrm(ks[15], (DEPTH, D_MODEL, D_FF), jnp.float32) * d_s,
        'w2_up': nrm(ks[16], (DEPTH, D_MODEL, D_FF), jnp.float32) * d_s,
        'w2_down': nrm(ks[17], (DEPTH, D_FF, D_MODEL), jnp.float32) * D_FF ** -0.5,
    }


def reference(x, c, w_mod, b_mod, g_ffn1, w1_gate, w1_up, w1_down, g_mix, w_in, b_f, g_q, g_k,
              w_o, g_ffn2, w2_gate, w2_up, w2_down):
    c_act = jax.nn.silu(c)
    for l in range(DEPTH):
        mod = c_act @ w_mod[l] + b_mod[l]
        sh1, sc1, ga1, sh2, sc2, ga2, sh3, sc3, ga3 = jnp.split(mod, N_MOD, axis=-1)
        h = modulate(rms_norm(x, g_ffn1[l]), sh1, sc1)
        x = x + 0.5 * (1 + ga1[:, None, :]) * swiglu(h, w1_gate[l], w1_up[l], w1_down[l])
        h = modulate(rms_norm(x, g_mix[l]), sh2, sc2)
        x = x + (1 + ga2[:, None, :]) * hybrid_mixer(h, w_in[l], b_f[l], g_q[l], g_k[l], w_o[l])
        h = modulate(rms_norm(x, g_ffn2[l]), sh3, sc3)
        x = x + 0.5 * (1 + ga3[:, None, :]) * swiglu(h, w2_gate[l], w2_up[l], w2_down[l])
    return x
```

```python
from contextlib import ExitStack
import os
import numpy as np
import concourse.bass as bass
import concourse.mybir as mybir
from concourse.bass_utils import run_bass_kernel_spmd

F32 = mybir.dt.float32
BF16 = mybir.dt.bfloat16
AF = mybir.ActivationFunctionType
ALU = mybir.AluOpType

SEG = 30000
EMBED_WAIT = os.environ.get("EMBED_WAIT", "1") == "1"
S_TOK = 4096
DM = 1024
DFF = 2816
NJ = 22
NEG = -30000.0
EPS = 1e-6
ARENA_BYTES = 212800


class Buf:
    __slots__ = ("name", "last_w", "readers", "dcount", "excl")

    def __init__(self, name, excl=False):
        self.name = name
        self.last_w = None
        self.readers = {}
        self.dcount = 0
        self.excl = excl

    def pending(self):
        ops = list(self.readers.values())
        if self.last_w is not None:
            ops.append(self.last_w)
        return ops


class Op:
    __slots__ = ("eng", "fn", "deps", "flag", "is_dma", "anchor", "val", "gidx")


class Sched:
    ENGS = ["pe", "act", "dve", "pool", "sp"]

    def __init__(self, nc):
        self.nc = nc
        self.ops = {e: [] for e in self.ENGS}
        self.anchors = []
        self.nops = 0

    def op(self, eng, fn, reads=(), writes=(), dma=None):
        o = Op()
        o.eng = eng
        o.fn = fn
        o.flag = False
        o.is_dma = dma is not None
        o.anchor = dma
        o.val = None
        o.gidx = self.nops
        self.nops += 1
        deps = {}

        def add(d, kind):
            if d is None:
                return
            if not o.is_dma and not d.is_dma and d.eng == eng:
                if eng == "pe" or kind == "rar":
                    return
            if d.is_dma and o.is_dma and d.anchor is o.anchor and d.eng == eng and kind == "waw":
                return
            deps[id(d)] = d

        for b in reads:
            add(b.last_w, "raw")
            if b.excl:
                for r in b.readers.values():
                    add(r, "rar")
        for b in writes:
            add(b.last_w, "waw")
            for r in b.readers.values():
                add(r, "war")
        o.deps = list(deps.values())
        for d in o.deps:
            d.flag = True
        for b in reads:
            key = ("dma", o.gidx) if o.is_dma else eng
            b.readers[key] = o
        for b in writes:
            b.last_w = o
            b.readers = {}
        if o.is_dma:
            if dma not in self.anchors:
                self.anchors.append(dma)
            dma.dcount += 16
            o.val = dma.dcount
            o.flag = True
        self.ops[eng].append(o)
        return o

    def emit(self):
        nc = self.nc
        nseg = {}
        for e in self.ENGS:
            c = 0
            for o in self.ops[e]:
                if not o.is_dma and o.flag:
                    c += 1
                    o.val = c
            nseg[e] = (c + SEG - 1) // SEG if c else 0
        with ExitStack() as st:
            esem = {e: [st.enter_context(nc.semaphore(f"s_{e}{i}")) for i in range(nseg[e])] for e in self.ENGS}
            dsem = {id(a): st.enter_context(nc.semaphore(f"d{k}")) for k, a in enumerate(self.anchors)}
            block = st.enter_context(nc.Block())

            def semval(d):
                if d.is_dma:
                    return dsem[id(d.anchor)], d.val, ("d", id(d.anchor))
                s = (d.val - 1) // SEG
                return esem[d.eng][s], d.val - s * SEG, ("e", d.eng)

            def run(e, engine):
                waited = {}
                for o in self.ops[e]:
                    need = {}
                    for d in o.deps:
                        _, _, key = semval(d)
                        if waited.get(key, 0) >= d.val:
                            continue
                        if key not in need or need[key].val < d.val:
                            need[key] = d
                    items = list(need.items())
                    embed = None
                    if EMBED_WAIT and o.fn is not None and not o.is_dma and items:
                        embed = items.pop()
                    for key, d in items:
                        s, v, _ = semval(d)
                        engine.wait_ge(s, v)
                        waited[key] = d.val
                    if o.fn is None:
                        continue
                    ins = o.fn(engine)
                    if embed is not None:
                        s, v, _ = semval(embed[1])
                        ins._wait_ge(s, v)
                        waited[embed[0]] = embed[1].val
                    if o.is_dma:
                        ins.then_inc(dsem[id(o.anchor)], 16)
                    elif o.flag:
                        s, _, _ = semval(o)
                        ins.then_inc(s, 1)

            block.tensor(lambda eng: run("pe", eng))
            block.scalar(lambda eng: run("act", eng))
            block.vector(lambda eng: run("dve", eng))
            block.gpsimd(lambda eng: run("pool", eng))
            block.sync(lambda eng: run("sp", eng))


class Tile:
    __slots__ = ("ap", "off", "size", "bufs", "name")

    @property
    def b(self):
        return self.bufs[0]


class Arena:
    def __init__(self, nc, st, nbytes):
        self.t = st.enter_context(nc.sbuf_tensor("arena", [128, nbytes // 2], BF16))
        self.nbytes = nbytes
        self.tiles = []

    def at(self, name, off, cols, dt, nbuf=1):
        esz = 4 if dt == F32 else 2
        size = cols * esz
        assert off % 4 == 0 and off + size <= self.nbytes, (name, off, size)
        ap = self.t[:, off // 2: off // 2 + size // 2]
        if dt == F32:
            ap = ap.bitcast(F32)
        t = Tile()
        t.name = name
        t.ap = ap
        t.off = off
        t.size = size
        t.bufs = [Buf(f"{name}{i}") for i in range(nbuf)]
        inh = {}
        for o in self.tiles:
            if o.off < off + size and off < o.off + o.size:
                for b in o.bufs:
                    for p in b.pending():
                        inh[id(p)] = p
        for b in t.bufs:
            for k, p in inh.items():
                b.readers[("inh", k)] = p
        self.tiles.append(t)
        return t


def _consts():
    idn = np.eye(128, dtype=np.float32)
    j = np.arange(128)[:, None]
    s = np.arange(128)[None, :]
    negU = np.where(j >= s, -1.0, 0.0).astype(np.float32)
    blk = np.where((j // 64) == (s // 64), 1.0 / 64.0, 0.0).astype(np.float32)
    diagS = np.where(j < s, 0.0, NEG).astype(np.float32)
    diagF = np.where(j <= s, 0.0, NEG).astype(np.float32)
    negcol = np.full((128, 384), NEG, np.float32)
    zer = np.zeros((128, 384), np.float32)
    mS = np.concatenate([negcol, diagS, zer], 1)
    mF = np.concatenate([negcol, diagF, zer], 1)
    cbf = np.concatenate([idn, negU, blk, mS, mF], 1)
    triL = np.where(j <= s, -1.0, 0.0).astype(np.float32)
    sel = np.zeros((128, 8, 128), np.float32)
    for h in range(8):
        sel[h, h, 64] = 1.0
        sel[32 + h, h, 96] = 1.0
    cf32 = np.concatenate([idn, triL, sel.reshape(128, 1024)], 1)
    return np.ascontiguousarray(cbf), np.ascontiguousarray(cf32)


PAIR_ORDER = [0, 4, 5, 6, 7, 1, 2, 3]


def build(stage=99, dbg=False):
    nc = bass.Bass("TRN2", target_bir_lowering=False)

    def din(name, shape, dt=F32):
        return nc.dram_tensor(name, shape, dt, kind="ExternalInput").ap()

    x_d = din("x", [S_TOK, DM])
    c_d = din("c", [1, DM])
    wmod_d = din("w_mod", [DM, 9 * DM])
    bmod_d = din("b_mod", [1, 9 * DM])
    g1_d = din("g_ffn1", [1, DM])
    w1g_d = din("w1_gate", [DM, DFF])
    w1u_d = din("w1_up", [DM, DFF])
    w1d_d = din("w1_down", [DFF, DM])
    gm_d = din("g_mix", [1, DM])
    win_d = din("w_in", [DM, 3080])
    bf_d = din("b_f", [1, 8])
    gq_d = din("g_q", [1, 512])
    gk_d = din("g_k", [1, 512])
    wo_d = din("w_o", [DM, DM])
    g2_d = din("g_ffn2", [1, DM])
    w2g_d = din("w2_gate", [DM, DFF])
    w2u_d = din("w2_up", [DM, DFF])
    w2d_d = din("w2_down", [DFF, DM])
    cbf_d = din("cbf", [128, 2176])
    cf32_d = din("cf32", [128, 1280])
    out_d = nc.dram_tensor("out", [S_TOK, DM], F32, kind="ExternalOutput").ap()
    skind = "ExternalOutput" if dbg else "Internal"
    X1_d = nc.dram_tensor("X1s", [S_TOK, DM], F32, kind=skind).ap()
    AT_d = nc.dram_tensor("ATs", [8, 128, S_TOK], BF16, kind=skind).ap()
    if dbg:
        H2_d = nc.dram_tensor("H2s", [128, 8 * S_TOK], BF16, kind="ExternalOutput").ap()
        MOD_d = nc.dram_tensor("MODs", [128, 72], F32, kind="ExternalOutput").ap()

    S = Sched(nc)
    with ExitStack() as st:
        A = Arena(nc, st, ARENA_BYTES)
        PS = [st.enter_context(nc.psum_tensor(f"ps{i}", [128, 512], F32)) for i in range(8)]
        PB = [Buf(f"ps{i}", excl=True) for i in range(8)]

        def psf(i):
            return PS[i][:, :]

        def psb16(i):
            return PS[i][:, :].bitcast(BF16)

        def MM(out, lhsT, rhs, start, stop, reads, writes, nochk=False):
            if nochk:
                S.op("pe", lambda e: e.matmul(out, lhsT=lhsT, rhs=rhs, start=start, stop=stop, skip_group_check=True), reads, writes)
            else:
                S.op("pe", lambda e: e.matmul(out, lhsT=lhsT, rhs=rhs, start=start, stop=stop), reads, writes)

        def TR(out, in_, ident, reads, writes):
            S.op("pe", lambda e: e.transpose(out, in_, ident), reads, writes)

        def ACT(out, in_, func, reads, writes, bias=None, scale=None, accum=None):
            kw = {}
            if bias is not None:
                kw["bias"] = bias
            if scale is not None:
                kw["scale"] = scale
            if accum is not None:
                kw["accum_out"] = accum
            S.op("act", lambda e: e.activation(out=out, in_=in_, func=func, **kw), reads, writes)

        def TT(eng, out, in0, in1, op, reads, writes):
            S.op(eng, lambda e: e.tensor_tensor(out=out, in0=in0, in1=in1, op=op), reads, writes)

        def TS(eng, out, in0, s1, s2, op0, op1, reads, writes):
            if s2 is None:
                S.op(eng, lambda e: e.tensor_scalar(out=out, in0=in0, scalar1=s1, scalar2=None, op0=op0), reads, writes)
            else:
                S.op(eng, lambda e: e.tensor_scalar(out=out, in0=in0, scalar1=s1, scalar2=s2, op0=op0, op1=op1), reads, writes)

        def STT(eng, out, in0, scalar, in1, op0, op1, reads, writes):
            S.op(eng, lambda e: e.scalar_tensor_tensor(out=out, in0=in0, scalar=scalar, in1=in1, op0=op0, op1=op1), reads, writes)

        def CP(eng, out, in_, reads, writes):
            S.op(eng, lambda e: e.tensor_copy(out=out, in_=in_), reads, writes)

        def MS(eng, ap, val, writes):
            S.op(eng, lambda e: e.memset(ap, val), (), writes)

        def DMA(q, out, in_, reads, writes, anchor, slow=False):
            if slow:
                S.op(q, lambda e: e.dma_start(out=out, in_=in_, allow_slow_non_contiguous=True), reads, writes, dma=anchor)
            else:
                S.op(q, lambda e: e.dma_start(out=out, in_=in_), reads, writes, dma=anchor)

        top = ARENA_BYTES
        def ptile(name, cols, dt, nbuf=1):
            nonlocal top
            esz = 4 if dt == F32 else 2
            top -= ((cols * esz + 31) // 32) * 32
            return A.at(name, top, cols, dt, nbuf)

        cbf = ptile("cbf", 128, BF16)
        ident_bf = cbf.ap[:, 0:128]
        ones32 = ptile("ones32", 128, F32)
        cf_i = ptile("cf_i", 256, F32)
        ident32 = cf_i.ap[:, 0:128]
        triL32 = cf_i.ap[:, 128:256]
        modT = ptile("modT", 72, F32)
        bmodT = ptile("bmodT", 72, F32)
        gvec = ptile("gvec", 24, F32)
        cT = ptile("cT", 8, F32)
        cactT = ptile("cactT", 8, BF16)
        gmul = ptile("gmul", 24, F32)
        gate = ptile("gate", 24, F32)
        gqk = ptile("gqk", 8, F32)
        bfB = ptile("bfB", 8, F32)
        ss = ptile("ss", 8, F32, nbuf=2)
        epsT = ptile("epsT", 8, F32)
        MS("dve", epsT.ap, EPS, [epsT.b])
        PERS = top

        DMA("pool", cbf.ap, cbf_d[:, 0:128], (), [cbf.b], cbf.b)
        DMA("sp", cf_i.ap, cf32_d[:, 0:256], (), [cf_i.b], cf_i.b)
        MS("dve", ones32.ap, 1.0, [ones32.b])
        DMA("sp", bmodT.ap, bmod_d.rearrange("o (c p) -> p (o c)", p=128), (), [bmodT.b], bmodT.b, slow=True)
        DMA("sp", cT.ap, c_d.rearrange("o (c p) -> p (o c)", p=128), (), [cT.b], cT.b, slow=True)
        for k, gd in enumerate([g1_d, gm_d, g2_d]):
            DMA("sp", gvec.ap[:, k * 8:(k + 1) * 8], gd.rearrange("o (c p) -> p (o c)", p=128), (), [gvec.b], gvec.b, slow=True)
        DMA("sp", gqk.ap[:, 0:4], gq_d.rearrange("o (c p) -> p (o c)", p=128), (), [gqk.b], gqk.b, slow=True)
        DMA("sp", gqk.ap[:, 4:8], gk_d.rearrange("o (c p) -> p (o c)", p=128), (), [gqk.b], gqk.b, slow=True)
        DMA("sp", bfB.ap, bf_d.partition_broadcast(128), (), [bfB.b], bfB.b)
        TS("dve", gqk.ap[:, 0:4], gqk.ap[:, 0:4], 0.125, None, ALU.mult, None, [gqk.b], [gqk.b])
        ACT(cactT.ap, cT.ap, AF.Silu, [cT.b], [cactT.b])

        KB = 1024
        W_OFF = 0
        R_OFF = 132 * KB
        XRES_OFF = R_OFF
        XN_OFF = XRES_OFF + 16 * KB
        HN_OFF = XN_OFF + 8 * KB
        HT_OFF = HN_OFF + 8 * KB
        AT_OFF = HT_OFF + 8 * KB
        SG_OFF = AT_OFF + 22 * KB
        R_END = SG_OFF + 2 * KB
        assert R_END <= PERS, (R_END, PERS)

        WM_OFF = 196 * KB
        assert WM_OFF + 8 * KB <= PERS, PERS
        wm = A.at("wm", WM_OFF, 4096, BF16, nbuf=2)
        wmod_v = wmod_d.rearrange("(kc p) n -> p kc n", p=128)

        def mod_group(g, bank=7):
            buf = wm.bufs[g % 2]
            wap = wm.ap[:, (g % 2) * 2048:(g % 2 + 1) * 2048].rearrange("p (kc n) -> p kc n", kc=8)
            DMA("pool", wap, wmod_v[:, :, g * 256:(g + 1) * 256], (), [buf], buf)
            for m in range(2):
                col = g * 2 + m
                for kc in range(8):
                    MM(PS[bank][:, col:col + 1], wap[:, kc, m * 128:(m + 1) * 128], cactT.ap[:, kc:kc + 1],
                       kc == 0, kc == 7, [buf, cactT.b], [PB[bank]])
            TT("dve", modT.ap[:, g * 2:g * 2 + 2], PS[bank][:, g * 2:g * 2 + 2], bmodT.ap[:, g * 2:g * 2 + 2], ALU.add,
               [PB[bank], bmodT.b], [modT.b])

        def mod_derive(k):
            sc = modT.ap[:, 24 * k + 8:24 * k + 16]
            STT("dve", gmul.ap[:, 8 * k:8 * k + 8], sc, 1.0, gvec.ap[:, 8 * k:8 * k + 8], ALU.add, ALU.mult,
                [modT.b, gvec.b], [gmul.b])

        def gate_derive(k):
            ga = modT.ap[:, 24 * k + 16:24 * k + 24]
            if k == 1:
                TS("dve", gate.ap[:, 8:16], ga, 1.0, None, ALU.add, None, [modT.b], [gate.b])
            else:
                TS("dve", gate.ap[:, 8 * k:8 * k + 8], ga, 0.5, 0.5, ALU.mult, ALU.add, [modT.b], [gate.b])

        def gate_bcast_tile(k, dst, dst_buf, tmp, tmp_buf, banks):
            for hh in range(2):
                bk = banks[hh]
                for c4 in range(4):
                    c = hh * 4 + c4
                    TS("dve", tmp.ap[:, (c % 2) * 128:(c % 2 + 1) * 128], ident32, gate.ap[:, 8 * k + c:8 * k + c + 1], None,
                       ALU.mult, None, [cf_i.b, gate.b], [tmp_buf[c % 2]])
                    MM(PS[bk][:, c4 * 128:(c4 + 1) * 128], ones32.ap, tmp.ap[:, (c % 2) * 128:(c % 2 + 1) * 128],
                       True, True, [ones32.b, tmp_buf[c % 2]], [PB[bk]])
                CP("dve", dst[:, hh * 512:(hh + 1) * 512], psf(bk), [PB[bk]], [dst_buf])

        for g in range(12):
            mod_group(g)
        mod_derive(0)
        gate_derive(0)

        JG = [(0, 6), (6, 12), (12, 17), (17, 22)]

        def ffn_weights(tag, wg_d, wu_d, wd_d, with_wd=True):
            Wg = A.at(f"Wg{tag}", W_OFF, 8 * DFF, BF16, nbuf=4)
            Wu = A.at(f"Wu{tag}", W_OFF + 44 * KB, 8 * DFF, BF16, nbuf=4)
            Wg3 = Wg.ap.rearrange("p (kc n) -> p kc n", kc=8)
            Wu3 = Wu.ap.rearrange("p (kc n) -> p kc n", kc=8)
            wgv = wg_d.rearrange("(kc p) n -> p kc n", p=128)
            wuv = wu_d.rearrange("(kc p) n -> p kc n", p=128)
            loads = []
            for gi, (j0, j1) in enumerate(JG):
                loads.append(lambda gi=gi, j0=j0, j1=j1: DMA("pool", Wg3[:, :, j0 * 128:j1 * 128], wgv[:, :, j0 * 128:j1 * 128], (), [Wg.bufs[gi]], Wg.bufs[gi]))
                loads.append(lambda gi=gi, j0=j0, j1=j1: DMA("pool", Wu3[:, :, j0 * 128:j1 * 128], wuv[:, :, j0 * 128:j1 * 128], (), [Wu.bufs[gi]], Wu.bufs[gi]))
            W = dict(Wg=Wg, Wu=Wu, Wg3=Wg3, Wu3=Wu3, loads=loads, tag=tag, wd_d=wd_d)
            if with_wd:
                ffn_weights_wd(W)
            return W

        def ffn_weights_wd(W):
            Wd = A.at(f"Wd{W['tag']}", W_OFF + 88 * KB, NJ * DM, BF16, nbuf=2)
            Wd3 = Wd.ap.rearrange("p (j n) -> p j n", j=NJ)
            wdv = W["wd_d"].rearrange("(j p) n -> p j n", p=128)
            for hh in range(2):
                W["loads"].append(lambda hh=hh: DMA("pool", Wd3[:, hh * 11:(hh + 1) * 11, :], wdv[:, hh * 11:(hh + 1) * 11, :], (), [Wd.bufs[hh]], Wd.bufs[hh]))
            W["Wd"] = Wd
            W["Wd3"] = Wd3

        def fold_gate(W, k, gb_ap, gb_buf):
            for hh in range(2):
                for q in range(11):
                    j = hh * 11 + q
                    TT("pool", W["Wd3"][:, j, :], W["Wd3"][:, j, :], gb_ap, ALU.mult, [W["Wd"].bufs[hh], gb_buf], [W["Wd"].bufs[hh]])

        def jgroup(j):
            for gi, (j0, j1) in enumerate(JG):
                if j0 <= j < j1:
                    return gi

        tr_ctr = [0]

        def transpose_block(hn_ap, hn_buf, k, dstT3, dst_buf, tcol0, banks=(4, 5)):
            bk = banks[tr_ctr[0] % len(banks)]
            tr_ctr[0] += 1
            pv = psb16(bk)
            for fc in range(8):
                TR(pv[:, fc * 128:(fc + 1) * 128], hn_ap[:, fc * 128:(fc + 1) * 128], ident_bf, [hn_buf, cbf.b], [PB[bk]])
            for fc in range(8):
                TS("dve", dstT3[:, fc, tcol0:tcol0 + 128], pv[:, fc * 128:(fc + 1) * 128],
                   gmul.ap[:, 8 * k + fc:8 * k + fc + 1], modT.ap[:, 24 * k + fc:24 * k + fc + 1], ALU.mult, ALU.add,
                   [PB[bk], gmul.b, modT.b], [dst_buf])

        nb_ctr = [0]

        def norm_block(src_rows, src_bufs, xn_t, hap, hb):
            slot = nb_ctr[0] % 2
            nb_ctr[0] += 1
            xb = xn_t.bufs[slot]
            xap = xn_t.ap[:, slot * 1024:(slot + 1) * 1024]
            DMA("sp", xap, src_rows, src_bufs, [xb], xb)
            sb_ = ss.bufs[slot]
            sc_ap = ss.ap[:, slot * 4:slot * 4 + 1]
            ln_ap = ss.ap[:, slot * 4 + 1:slot * 4 + 2]
            rs_ap = ss.ap[:, slot * 4 + 2:slot * 4 + 3]
            MS("dve", sc_ap, 0.0, [sb_])
            ACT(hap, xap, AF.Square, [xb, sb_], [hb, sb_], accum=sc_ap)
            ACT(ln_ap, sc_ap, AF.Ln, [sb_], [sb_], bias=epsT.ap[:, 0:1], scale=1.0 / DM)
            ACT(rs_ap, ln_ap, AF.Exp, [sb_], [sb_], scale=-0.5)
            TS("dve", hap, xap, rs_ap, None, ALU.mult, None, [xb, sb_], [hb])

        def ffn_phase(tag, src_d, src_bufs, dst_d, dst_bufs, W, k, sg, inter=None):
            xn = A.at(f"xn{tag}", XN_OFF, 2048, F32, nbuf=2)
            hn = A.at(f"hn{tag}", HN_OFF, 4096, BF16, nbuf=4)
            hT = A.at(f"hT{tag}", HT_OFF, 4096, BF16)
            hT3 = hT.ap.rearrange("p (kc t) -> p kc t", kc=8)
            hn3 = hn.ap.rearrange("p (b n) -> p b n", b=4)
            ATt = A.at(f"AT{tag}", AT_OFF, NJ * 512, BF16, nbuf=NJ)
            AT3 = ATt.ap.rearrange("p (j t) -> p j t", j=NJ)
            xres = None

            def norm_part(i, b):
                blk = i * 4 + b
                norm_block(src_d[blk * 128:(blk + 1) * 128, :], [src_bufs[i]] if src_bufs else [], xn, hn3[:, b, :], hn.bufs[b])

            def tr_part(i, b):
                transpose_block(hn3[:, b, :], hn.bufs[b], k, hT3, hT.b, b * 128)

            EARLY = os.environ.get("EARLYNORM", "1") == "1"
            for b in range(4):
                norm_part(0, b)
                tr_part(0, b)
            for i in range(8):
                for j in range(NJ):
                    gi = jgroup(j)
                    for kc in range(8):
                        MM(psf(j % 2), W["Wg3"][:, kc, j * 128:(j + 1) * 128], hT3[:, kc, :], kc == 0, kc == 7,
                           [W["Wg"].bufs[gi], hT.b], [PB[j % 2]])
                    for kc in range(8):
                        MM(psf(2 + j % 2), W["Wu3"][:, kc, j * 128:(j + 1) * 128], hT3[:, kc, :], kc == 0, kc == 7,
                           [W["Wu"].bufs[gi], hT.b], [PB[2 + j % 2]])
                    sgap = sg.ap[:, (j % 2) * 512:(j % 2 + 1) * 512]
                    ACT(sgap, psf(j % 2), AF.Silu, [PB[j % 2]], [sg.bufs[j % 2]])
                    TT("dve", AT3[:, j, :], psf(2 + j % 2), sgap, ALU.mult, [PB[2 + j % 2], sg.bufs[j % 2]], [ATt.bufs[j]])
                    if inter is not None:
                        inter(i, j)
                    if EARLY and i + 1 < 8 and j in (3, 8, 13, 18):
                        norm_part(i + 1, (j - 3) // 5)
                if i + 1 < 8:
                    for b in range(4):
                        if not EARLY:
                            norm_part(i + 1, b)
                        tr_part(i + 1, b)
                if xres is None:
                    xres = A.at(f"xres{tag}", XRES_OFF, 4096, F32)
                    xres3 = xres.ap.rearrange("p (b n) -> p b n", b=4)
                DMA("sp", xres3, src_d[i * 512:(i + 1) * 512, :].rearrange("(b p) n -> p b n", p=128),
                    [src_bufs[i]] if src_bufs else [], [xres.b], xres.b)
                for b in range(4):
                    for nh in range(2):
                        bk = 6 + (b * 2 + nh) % 2
                        for j in range(NJ):
                            MM(psf(bk), AT3[:, j, b * 128:(b + 1) * 128], W["Wd3"][:, j, nh * 512:(nh + 1) * 512], j == 0, j == NJ - 1,
                               [ATt.bufs[j], W["Wd"].bufs[j // 11]], [PB[bk]])
                        TT("dve", xres3[:, b, nh * 512:(nh + 1) * 512], psf(bk), xres3[:, b, nh * 512:(nh + 1) * 512], ALU.add,
                           [PB[bk], xres.b], [xres.b])
                DMA("sp", dst_d[i * 512:(i + 1) * 512, :].rearrange("(b p) n -> p b n", p=128), xres3, [xres.b],
                    [dst_bufs[i]] if dst_bufs else [], xres.b)
            return xres

        W1 = ffn_weights("1", w1g_d, w1u_d, w1d_d)
        for ld in W1["loads"]:
            ld()
        gb1 = A.at("gb1", XRES_OFF, 1024, F32)
        dtmp = A.at("dtmp", XRES_OFF + 4 * KB, 256, F32, nbuf=2)
        gate_bcast_tile(0, gb1.ap, gb1.b, dtmp, dtmp.bufs, [2, 3])
        fold_gate(W1, 0, gb1.ap, gb1.b)
        sg1 = A.at("sg1", SG_OFF, 1024, BF16, nbuf=2)

        pending_groups = list(range(12, 36))

        def inter1(i, j):
            if i in (1, 2) and j % 2 == 0 and pending_groups:
                mod_group(pending_groups.pop(0))

        x1b = [Buf(f"X1d{i}") for i in range(8)]
        final_bufs = []
        if stage >= 1 and not os.environ.get('SKIPFFN1'):
            xr = ffn_phase("1", x_d, None, X1_d, x1b, W1, 0, sg1, inter=inter1)
            if stage == 1:
                final_bufs.append(xr.b)
        while pending_groups:
            mod_group(pending_groups.pop(0))
        mod_derive(1)
        mod_derive(2)
        gate_derive(1)
        gate_derive(2)
        if dbg:
            DMA("sp", MOD_d, modT.ap, [modT.b], [], modT.b)
            final_bufs.append(modT.b)

        atb = [Buf(f"ATd{i}") for i in range(8)]
        if stage >= 2:
            SLOT_OFF = [64 * KB, 89 * KB]
            QT_OFF = 114 * KB
            WC_OFF = 118 * KB
            WF_OFF = 130 * KB
            FT_OFF = 131 * KB
            DF_OFF = 134 * KB
            BQ_OFF = 142 * KB
            CR_OFF = 144 * KB
            TMP_OFF = 152 * KB
            ATT_OFF = 184 * KB
            assert ATT_OFF + 8 * KB <= PERS, (ATT_OFF, PERS)
            win_v = win_d.rearrange("(kc p) n -> p kc n", p=128)
            CA_OFF = 193 * KB
            cbfA = A.at("cbfA", CA_OFF, 2048, BF16)
            negU_bf = cbfA.ap[:, 0:128]
            blk_bf = cbfA.ap[:, 128:256]
            mS_bf = cbfA.ap[:, 256:1152]
            mF_bf = cbfA.ap[:, 1152:2048]
            cbf2 = A.at("cbf2", CA_OFF + 4 * KB, 1024, BF16)
            sel_bf = cbf2.ap
            negones = A.at("negones", CA_OFF + 6 * KB, 128, BF16)
            assert CA_OFF + 6 * KB + 256 <= PERS, PERS
            DMA("pool", cbfA.ap, cbf_d[:, 128:2176], (), [cbfA.b], cbfA.b)
            DMA("pool", cbf2.ap, cf32_d[:, 256:1280], (), [cbf2.b], cbf2.b)
            MS("pool", negones.ap, -1.0, [negones.b])

            h2T = A.at("h2T", 0, 8 * S_TOK, BF16, nbuf=8)
            h2T3 = h2T.ap.rearrange("p (kc t) -> p kc t", kc=8)
            xn_b = A.at("xn_b", SLOT_OFF[1], 4096, F32, nbuf=4)
            hn_b = A.at("hn_b", SLOT_OFF[1] + 16 * KB, 4096, BF16, nbuf=4)
            ss_b = A.at("ss_b", SLOT_OFF[1] + 24 * KB, 16, F32, nbuf=4)
            qtiles = [[A.at(f"q{s_}{x_}", QT_OFF + s_ * 2 * KB + x_ * KB, 512, BF16) for x_ in range(2)] for s_ in range(2)]
            wcs = [A.at(f"wc{s_}", WC_OFF + s_ * 6 * KB, 3 * 1024, BF16, nbuf=3) for s_ in range(2)]
            wf = A.at("wf", WF_OFF, 64, BF16)
            fT = A.at("fT", FT_OFF, 256, F32)
            spT = A.at("spT", FT_OFF + KB, 256, F32)
            lfcT = A.at("lfcT", FT_OFF + 2 * KB, 256, F32)
            dF = A.at("dF", DF_OFF, 4096, BF16, nbuf=8)
            biasq = A.at("biasq", BQ_OFF, 512, F32, nbuf=2)
            crefB = A.at("crefB", CR_OFF, 64, F32)
            totT = A.at("totT", CR_OFF + 256, 256, F32)
            offsT = A.at("offsT", CR_OFF + 256 + KB, 256, F32)
            r32 = A.at("r32", CR_OFF + 256 + 2 * KB, 512, F32)
            d32 = A.at("d32", CR_OFF + 256 + 4 * KB, 512, F32)
            hib = A.at("hib", CR_OFF + 256 + 6 * KB, 512, BF16)
            t_ = TMP_OFF
            Esb = A.at("Esb", t_, 1024, F32, nbuf=2); t_ += 4 * KB
            SPsb = A.at("SPsb", t_, 1024, BF16, nbuf=2); t_ += 2 * KB
            tmps = A.at("tmps", t_, 1024, F32, nbuf=2); t_ += 4 * KB
            wsb = A.at("wsb", t_, 1024, BF16, nbuf=2); t_ += 2 * KB
            Rsb = A.at("Rsb", t_, 1024, F32, nbuf=2); t_ += 4 * KB
            Psb = A.at("Psb", t_, 1536, BF16, nbuf=3); t_ += 3 * KB
            osb = A.at("osb", t_, 1024, F32, nbuf=2); t_ += 4 * KB
            rbs = A.at("rbs", t_, 1024, F32, nbuf=2); t_ += 4 * KB
            q32 = A.at("q32", t_, 512, F32); t_ += 2 * KB
            sqb = A.at("sqb", t_, 512, BF16); t_ += 1 * KB
            rsq = A.at("rsq", t_, 512, F32); t_ += 2 * KB
            assert t_ <= ATT_OFF, (t_, ATT_OFF)
            atT = A.at("atT", ATT_OFF, S_TOK, BF16)
            fT3 = fT.ap.rearrange("p (b h) -> p b h", h=8)
            spT3 = spT.ap.rearrange("p (b h) -> p b h", h=8)
            lfcT3 = lfcT.ap.rearrange("p (b h) -> p b h", h=8)
            totT3 = totT.ap.rearrange("p (b h) -> p b h", h=8)
            offsT3 = offsT.ap.rearrange("p (b h) -> p b h", h=8)
            crefB3 = crefB.ap.rearrange("p (q h) -> p q h", h=8)

            DMA("pool", wf.ap.rearrange("p (kc n) -> p kc n", kc=8), win_v[:, :, 3072:3080], (), [wf.b], wf.b, slow=True)
            wf3 = wf.ap.rearrange("p (kc n) -> p kc n", kc=8)
            MS("pool", dF.ap[0:64, :], 0.0, dF.bufs)

            nb2 = [0]

            n2bank = [0]
            UPFRONT = os.environ.get("UPFRONT", "1") == "1"

            def norm2_minis(i):
                out = []
                for b in range(4):
                    st_ = {}
                    def f1(b=b, st_=st_):
                        blk = i * 4 + b
                        slot = nb2[0] % 2
                        nb2[0] += 1
                        st_["hb"] = hn_b.bufs[slot]
                        st_["hap"] = hn_b.ap[:, slot * 1024:(slot + 1) * 1024]
                        norm_block(X1_d[blk * 128:(blk + 1) * 128, :], [x1b[i]], xn_b, st_["hap"], st_["hb"])
                    def f2(b=b, st_=st_):
                        n2bank[0] += 1
                        st_["bk"] = (4 + n2bank[0] % 4) if UPFRONT else 7
                        pv = psb16(st_["bk"])
                        for fc in range(8):
                            TR(pv[:, fc * 128:(fc + 1) * 128], st_["hap"][:, fc * 128:(fc + 1) * 128], ident_bf, [st_["hb"], cbf.b], [PB[st_["bk"]]])
                    def f3(b=b, st_=st_):
                        blk = i * 4 + b
                        pv = psb16(st_["bk"])
                        for fc in range(8):
                            if UPFRONT and fc % 2 == 1:
                                ACT(h2T3[:, fc, blk * 128:blk * 128 + 128], pv[:, fc * 128:(fc + 1) * 128], AF.Identity,
                                    [PB[st_["bk"]], gmul.b, modT.b], [h2T.bufs[i]],
                                    bias=modT.ap[:, 24 + fc:24 + fc + 1], scale=gmul.ap[:, 8 + fc:8 + fc + 1])
                            else:
                                TS("dve", h2T3[:, fc, blk * 128:blk * 128 + 128], pv[:, fc * 128:(fc + 1) * 128],
                                   gmul.ap[:, 8 + fc:8 + fc + 1], modT.ap[:, 24 + fc:24 + fc + 1], ALU.mult, ALU.add,
                                   [PB[st_["bk"]], gmul.b, modT.b], [h2T.bufs[i]])
                    out += [f1, f2, f3]
                return out

            def pair_cols(p):
                if p < 4:
                    return p * 128, 512 + p * 128, 1024 + p * 128
                lp = p - 4
                return 1536 + lp * 128, 2048 + lp * 128, 2560 + lp * 128

            pair_state = {}

            def pair_setup(a):
                p = PAIR_ORDER[a]
                sl = a % 2
                fox = p >= 4
                kA = A.at(f"kA{a}", SLOT_OFF[sl], 4096, BF16, nbuf=8)
                kB = A.at(f"kB{a}", SLOT_OFF[sl] + 8 * KB, 4096, BF16, nbuf=8)
                vw = 130 if fox else 128
                V = A.at(f"V{a}", SLOT_OFF[sl] + 16 * KB, 32 * vw + 64, BF16, nbuf=8)
                MS("pool", V.ap[:, 32 * vw:32 * vw + 64], 0.0, V.bufs)
                wc = wcs[sl]
                wc4 = wc.ap.rearrange("p (w kc n) -> p w kc n", w=3, kc=8)
                cq, ck, cv = pair_cols(p)
                for wi, c0 in enumerate((cq, ck, cv)):
                    DMA("pool", wc4[:, wi, :, :], win_v[:, :, c0:c0 + 128], (), [wc.bufs[wi]], wc.bufs[wi], slow=True)
                if fox:
                    for kt_ in (kA, kB):
                        MS("pool", kt_.ap[64:128, :], 0.0, kt_.bufs)
                        MS("pool", kt_.ap[64:65, :], 1.0, kt_.bufs)
                        MS("pool", kt_.ap[96:97, :], 1.0, kt_.bufs)
                    V4 = V.ap[:, 0:32 * 130].rearrange("p (b h e) -> p b h e", b=32, h=2)
                    MS("pool", V4[:, :, :, 64:65], 1.0, V.bufs)
                pair_state[a] = dict(p=p, fox=fox, kA=kA, kB=kB, V=V, wc=wc, wc4=wc4, vw=vw)

            mb_ctr = [0]

            def qkv_minis(a, i, banks=(7,)):
                stt = pair_state[a]
                p, fox, kA, kB, V, wc, wc4 = stt["p"], stt["fox"], stt["kA"], stt["kB"], stt["V"], stt["wc"], stt["wc4"]
                qA, qB = qtiles[i % 2]
                tsl = slice(i * 512, (i + 1) * 512)
                out = []
                lp = p % 4
                cur = [banks[0]]

                def nb():
                    mb_ctr[0] += 1
                    cur[0] = banks[mb_ctr[0] % len(banks)]
                    return cur[0]

                def proj_stage(wi):
                    def f():
                        bk = nb()
                        for kc in range(8):
                            MM(psf(bk), wc4[:, wi, kc, :], h2T3[:, kc, tsl], kc == 0, kc == 7, [wc.bufs[wi], h2T.bufs[i]], [PB[bk]])
                    return f

                if not fox:
                    out.append(proj_stage(0))
                    out.append(lambda: TS("dve", qA.ap, psf(cur[0]), 0.125, None, ALU.mult, None, [PB[cur[0]]], [qA.b]))
                    out.append(proj_stage(1))
                    out.append(lambda: CP("dve", kA.ap[:, tsl], psf(cur[0]), [PB[cur[0]]], [kA.bufs[i]]))
                else:
                    def normed(wi, gcol, dstA, dstA_buf, dstB, dstB_buf, dcols):
                        def s1():
                            CP("dve", q32.ap, psf(cur[0]), [PB[cur[0]]], [q32.b])
                        def s2():
                            ACT(sqb.ap, q32.ap, AF.Square, [q32.b], [sqb.b])
                        def s3():
                            bk = nb()
                            MM(psf(bk), blk_bf, sqb.ap, True, True, [cbfA.b, sqb.b], [PB[bk]])
                        def s4():
                            ACT(rsq.ap, psf(cur[0]), AF.Ln, [PB[cur[0]], epsT.b], [rsq.b], bias=epsT.ap[:, 0:1])
                            ACT(rsq.ap, rsq.ap, AF.Exp, [rsq.b], [rsq.b], scale=-0.5)
                        def s5():
                            STT("dve", dstA[0:64, dcols], q32.ap[0:64, :], gqk.ap[0:64, gcol:gcol + 1], rsq.ap[0:64, :], ALU.mult, ALU.mult,
                                [q32.b, gqk.b, rsq.b], [dstA_buf])
                            STT("dve", dstB[0:64, dcols], q32.ap[64:128, :], gqk.ap[64:128, gcol:gcol + 1], rsq.ap[64:128, :], ALU.mult, ALU.mult,
                                [q32.b, gqk.b, rsq.b], [dstB_buf])
                        return [proj_stage(wi), s1, s2, s3, s4, s5]
                    out += normed(0, lp, qA.ap, qA.b, qB.ap, qB.b, slice(0, 512))
                    for hh, qx in enumerate((qA, qB)):
                        def r1(hh=hh):
                            h = 2 * lp + hh
                            bk = nb()
                            MM(psf(bk), sel_bf[0:40, h * 128:(h + 1) * 128], dF.ap[0:40, tsl], True, True, [cbf2.b, dF.bufs[i]], [PB[bk]])
                        def r2(qx=qx):
                            CP("dve", qx.ap[64:97, :], PS[cur[0]][64:97, :], [PB[cur[0]]], [qx.b])
                        out += [r1, r2]
                    out += normed(1, 4 + lp, kA.ap, kA.bufs[i], kB.ap, kB.bufs[i], tsl)

                def v1():
                    bk = nb()
                    for b in range(4):
                        for kc in range(8):
                            MM(PS[bk][:, b * 128:(b + 1) * 128], h2T3[:, kc, i * 512 + b * 128:i * 512 + (b + 1) * 128], wc4[:, 2, kc, :],
                               kc == 0, kc == 7, [wc.bufs[2], h2T.bufs[i]], [PB[bk]])
                def v2():
                    bk = cur[0]
                    if fox:
                        V4 = V.ap[:, 0:32 * 130].rearrange("p (b h e) -> p b h e", b=32, h=2)
                        CP("dve", V4[:, 4 * i:4 * i + 4, :, 0:64], psf(bk).rearrange("p (b h e) -> p b h e", b=4, h=2), [PB[bk]], [V.bufs[i]])
                    else:
                        V3 = V.ap[:, 0:32 * 128].rearrange("p (b n) -> p b n", b=32)
                        CP("dve", V3[:, 4 * i:4 * i + 4, :], psf(bk).rearrange("p (b n) -> p b n", b=4), [PB[bk]], [V.bufs[i]])
                out += [v1, v2]
                return out

            def fproj_mini(i):
                def f():
                    for b in range(4):
                        for kc in range(8):
                            MM(PS[7][:, b * 8:(b + 1) * 8], h2T3[:, kc, i * 512 + b * 128:i * 512 + (b + 1) * 128], wf3[:, kc, :],
                               kc == 0, kc == 7, [wf.b, h2T.bufs[i]], [PB[7]])
                    CP("dve", fT.ap[:, i * 32:(i + 1) * 32], PS[7][:, 0:32], [PB[7]], [fT.b])
                return f

            def lfc_minis():
                out = []
                def l1():
                    TT("dve", fT3, fT3, bfB.ap.rearrange("p (o h) -> p o h", o=1).to_broadcast([128, 32, 8]), ALU.add, [fT.b, bfB.b], [fT.b])
                    ACT(spT.ap, fT.ap, AF.Exp, [fT.b], [spT.b], scale=-1.0)
                    ACT(spT.ap, spT.ap, AF.Ln, [spT.b], [spT.b], bias=1.0)
                    MM(PS[7][:, 0:256], triL32, spT.ap, True, True, [cf_i.b, spT.b], [PB[7]])
                    MM(PS[7][:, 256:512], ones32.ap, spT.ap, True, True, [ones32.b, spT.b], [PB[7]])
                    CP("dve", lfcT.ap, PS[7][:, 0:256], [PB[7]], [lfcT.b])
                    CP("dve", totT.ap, PS[7][:, 256:512], [PB[7]], [totT.b])
                    MS("dve", offsT3[:, 0, :], 0.0, [offsT.b])
                    for b in range(1, 32):
                        TT("dve", offsT3[:, b, :], offsT3[:, b - 1, :], totT3[:, b - 1, :], ALU.subtract, [offsT.b, totT.b], [offsT.b])
                    TT("dve", lfcT.ap, lfcT.ap, offsT.ap, ALU.add, [lfcT.b, offsT.b], [lfcT.b])
                    for qt in range(8):
                        MM(PS[7][:, qt * 8:(qt + 1) * 8], ones32.ap[0:1, :], lfcT3[0:1, 4 * qt, :], True, True, [ones32.b, lfcT.b], [PB[7]])
                    CP("dve", crefB.ap, PS[7][:, 0:64], [PB[7]], [crefB.b])
                out.append(l1)
                for i in range(8):
                    def lr(i=i):
                        for b in range(4):
                            TR(PS[7][0:8, b * 128:(b + 1) * 128], lfcT3[:, 4 * i + b, :], ident32, [lfcT.b, cf_i.b], [PB[7]])
                        CP("dve", r32.ap[0:8, :], PS[7][0:8, :], [PB[7]], [r32.b])
                        TS("dve", d32.ap[0:8, :], r32.ap[0:8, :], r32.ap[0:8, 0:1], None, ALU.subtract, None, [r32.b], [d32.b])
                        CP("dve", hib.ap[0:8, :], d32.ap[0:8, :], [d32.b], [hib.b])
                        CP("dve", r32.ap[0:8, :], hib.ap[0:8, :], [hib.b], [r32.b])
                        TT("dve", dF.ap[32:40, i * 512:(i + 1) * 512], d32.ap[0:8, :], r32.ap[0:8, :], ALU.subtract, [d32.b, r32.b], [dF.bufs[i]])
                        CP("dve", dF.ap[0:8, i * 512:(i + 1) * 512], hib.ap[0:8, :], [hib.b], [dF.bufs[i]])
                    out.append(lr)
                return out

            def make_units(qt):
                us = []
                for hh in range(2):
                    nkb = 4 * qt + 4
                    for idx, kb in enumerate(range(nkb - 1, -1, -1)):
                        j = kb - 4 * qt
                        first = idx == 0
                        c0 = 0 if j < 0 else j * 128
                        us.append(dict(qt=qt, hh=hh, kb=kb, j=j, first=first, last=(kb == 0), c0=c0, ch=qt * 2 + hh))
                return us

            def run_pair(a, groups):
                stt = pair_state[a]
                p, fox, kA, kB, V = stt["p"], stt["fox"], stt["kA"], stt["kB"], stt["V"]
                units = []
                gstart = []
                for qt in range(8):
                    gstart.append(len(units))
                    units += make_units(qt)
                N = len(units)
                sched_minis = {}
                pre = {}
                for qt in range(8):
                    n0 = gstart[qt]
                    n1 = gstart[qt + 1] if qt < 7 else N
                    cnt = max(1, n1 - n0 - (2 if fox else 1))
                    ms = groups[qt]
                    for k_, m in enumerate(ms):
                        it = n0 + min(cnt - 1, (k_ * cnt) // max(1, len(ms)))
                        sched_minis.setdefault(it, []).append(m)
                if fox:
                    V4 = V.ap[:, 0:32 * 130].rearrange("p (b h e) -> p b h e", b=32, h=2)
                    lp = p - 4
                else:
                    V3 = V.ap[:, 0:32 * 128].rearrange("p (b n) -> p b n", b=32)
                atT_rows = [slice(0, 64), slice(64, 128)]

                def qk(n):
                    u = units[n]
                    a3 = n % 3
                    c0, kb, j, hh, qt = u["c0"], u["kb"], u["j"], u["hh"], u["qt"]
                    qA, qB = qtiles[qt % 2]
                    ks = slice(kb * 128, (kb + 1) * 128)
                    if fox:
                        qx = (qA, qB)[hh]
                        kx = (kA, kB)[hh]
                        MM(PS[a3][:, c0:512], kx.ap[0:97, ks], qx.ap[0:97, c0:512], True, j < 0, [kx.bufs[kb // 4], qx.b], [PB[a3]])
                        mk = mF_bf
                    else:
                        hr = atT_rows[hh]
                        MM(PS[a3][:, c0:512], kA.ap[hr, ks], qA.ap[hr, c0:512], True, False, [kA.bufs[kb // 4], qA.b], [PB[a3]], nochk=True)
                        mk = mS_bf
                    if j >= 0:
                        m0 = (3 - j) * 128 + c0
                        m1 = (3 - j) * 128 + (j + 1) * 128
                        MM(PS[a3][:, c0:(j + 1) * 128], ident_bf, mk[:, m0:m1], False, fox, [cbf.b, cbfA.b], [PB[a3]], nochk=not fox)

                ND_SB = int(os.environ.get("DUMMY_SB", "0"))
                ND_FOX = int(os.environ.get("DUMMY_FOX", "0"))
                DUMN = int(os.environ.get("DUMMY_N", "128"))
                DBANK = 4

                def dummies(k):
                    for _ in range(k):
                        S.op("pe", lambda e: e.matmul(PS[DBANK][:, 0:DUMN], lhsT=ident_bf, rhs=mS_bf[:, 0:DUMN], start=True, stop=True, skip_group_check=True), [], [])

                if not fox:
                    def it_body(n):
                        if n + 1 < N:
                            qk(n + 1)
                        dummies(ND_SB)
                        if n < N:
                            u = units[n]
                            a3, s2 = n % 3, n % 2
                            c0, ch = u["c0"], u["ch"] % 2
                            cs = slice(c0, 512)
                            e_ap = Esb.ap[:, s2 * 512:(s2 + 1) * 512]
                            sp_ap = SPsb.ap[:, s2 * 512:(s2 + 1) * 512]
                            tm_ap = tmps.ap[:, s2 * 512:(s2 + 1) * 512]
                            r_ap = Rsb.ap[:, ch * 512:(ch + 1) * 512]
                            ACT(e_ap[:, cs], PS[a3][:, cs], AF.Exp, [PB[a3]], [Esb.bufs[s2]])
                            ACT(sp_ap[:, cs], e_ap[:, cs], AF.Ln, [Esb.bufs[s2]], [SPsb.bufs[s2]], bias=1.0)
                            MM(PS[a3][:, cs], negU_bf, sp_ap[:, cs], False, True, [cbfA.b, SPsb.bufs[s2]], [PB[a3]], nochk=True)
                            if not u["last"]:
                                MM(PS[3][:, cs], negones.ap, sp_ap[:, cs], True, True, [negones.b, SPsb.bufs[s2]], [PB[3]])
                            if u["first"]:
                                if not u["last"]:
                                    MS("pool", r_ap[:, 0:c0], 0.0, [Rsb.bufs[ch]])
                                    CP("dve", r_ap[:, cs], PS[3][:, cs], [PB[3]], [Rsb.bufs[ch]])
                            else:
                                TT("dve", tm_ap[:, cs], PS[a3][:, cs], r_ap[:, cs], ALU.add, [PB[a3], Rsb.bufs[ch]], [tmps.bufs[s2]])
                                if not u["last"]:
                                    TT("dve", r_ap[:, cs], PS[3][:, cs], r_ap[:, cs], ALU.add, [PB[3], Rsb.bufs[ch]], [Rsb.bufs[ch]])
                        m = n - 1
                        if 0 <= m < N:
                            u = units[m]
                            a3, s2 = m % 3, m % 2
                            c0, ch = u["c0"], u["ch"] % 2
                            cs = slice(c0, 512)
                            w_ap = wsb.ap[:, s2 * 512:(s2 + 1) * 512]
                            if u["first"]:
                                ACT(w_ap[:, cs], PS[a3][:, cs], AF.Exp, [PB[a3]], [wsb.bufs[s2]])
                            else:
                                ACT(w_ap[:, cs], tmps.ap[:, s2 * 512:(s2 + 1) * 512][:, cs], AF.Exp, [tmps.bufs[s2]], [wsb.bufs[s2]])
                            hh, kb, qt = u["hh"], u["kb"], u["qt"]
                            vo = kb * 128 + hh * 64
                            if u["first"]:
                                MM(PS[5 + ch][:, 0:512], mS_bf[:, 512:640], mS_bf[:, 0:512], True, False, [cbfA.b], [PB[5 + ch]])
                            MM(PS[5 + ch][:, cs], V.ap[:, vo:vo + 128], w_ap[:, cs], False, u["last"],
                               [V.bufs[kb // 4], V.bufs[min(7, (kb + 1) // 4)], wsb.bufs[s2]], [PB[5 + ch]])
                            if u["last"]:
                                CP("dve", atT.ap[atT_rows[hh], qt * 512:(qt + 1) * 512], PS[5 + ch][0:64, :], [PB[5 + ch]], [atT.b])
                else:
                    def it_body(n):
                        if n + 2 < N:
                            qk(n + 2)
                        dummies(ND_FOX)
                        if n < N:
                            u = units[n]
                            a3 = n % 3
                            cs = slice(u["c0"], 512)
                            bq = biasq.ap[:, (u["qt"] % 2) * 256:(u["qt"] % 2 + 1) * 256].rearrange("p (b h) -> p b h", h=8)
                            ACT(Psb.ap[:, a3 * 512:(a3 + 1) * 512][:, cs], PS[a3][:, cs], AF.Exp, [PB[a3], biasq.bufs[u["qt"] % 2]], [Psb.bufs[a3]],
                                bias=bq[:, u["kb"], u["hh"]:u["hh"] + 1])
                        m = n - 2
                        if 0 <= m < N:
                            u = units[m]
                            a3 = m % 3
                            cs = slice(u["c0"], 512)
                            ch = u["ch"] % 2
                            hh, kb, qt = u["hh"], u["kb"], u["qt"]
                            vo = kb * 130 + hh * 65
                            if u["first"]:
                                MM(PS[5 + ch][:, 0:512], mF_bf[:, 512:640], mF_bf[:, 0:512], True, False, [cbfA.b], [PB[5 + ch]])
                            MM(PS[5 + ch][:, cs], V.ap[:, vo:vo + 128], Psb.ap[:, a3 * 512:(a3 + 1) * 512][:, cs], False, u["last"],
                               [V.bufs[kb // 4], V.bufs[min(7, (kb + 1) // 4)], Psb.bufs[a3]], [PB[5 + ch]])
                            if u["last"]:
                                o_ap = osb.ap[:, ch * 512:(ch + 1) * 512]
                                ACT(o_ap[64:65, :], PS[5 + ch][64:65, :], AF.Ln, [PB[5 + ch]], [osb.bufs[ch]])
                                ACT(o_ap[64:65, :], o_ap[64:65, :], AF.Exp, [osb.bufs[ch]], [osb.bufs[ch]], scale=-1.0)
                        m = n - 4
                        if 0 <= m < N and units[m]["last"]:
                            u = units[m]
                            ch = u["ch"] % 2
                            hh, qt = u["hh"], u["qt"]
                            o_ap = osb.ap[:, ch * 512:(ch + 1) * 512]
                            rb_ap = rbs.ap[:, ch * 512:(ch + 1) * 512]
                            MM(PS[3][0:64, :], ones32.ap[64:65, 0:64], o_ap[64:65, :], True, True, [ones32.b, osb.bufs[ch]], [PB[3]])
                            CP("dve", rb_ap[0:64, :], PS[3][0:64, :], [PB[3]], [rbs.bufs[ch]])
                            TT("dve", atT.ap[atT_rows[hh], qt * 512:(qt + 1) * 512], PS[5 + ch][0:64, :], rb_ap[0:64, :], ALU.mult,
                               [PB[5 + ch], rbs.bufs[ch]], [atT.b])

                def group_start(qt):
                    if fox:
                        lp_ = p - 4
                        nkb = 4 * qt + 4
                        bq = biasq.ap[:, (qt % 2) * 256:(qt % 2 + 1) * 256].rearrange("p (b h) -> p b h", h=8)
                        STT("dve", bq[:, 0:nkb, 0:2], lfcT3[:, 0:nkb, 2 * lp_:2 * lp_ + 2], -1.0,
                            crefB3[:, qt:qt + 1, 2 * lp_:2 * lp_ + 2].to_broadcast([128, nkb, 2]), ALU.mult, ALU.add,
                            [lfcT.b, crefB.b], [biasq.bufs[qt % 2]])

                gs = set(gstart)
                group_start(0)
                qk(0)
                if fox:
                    qk(1)
                for n in range(N + (5 if fox else 1)):
                    if (n + 1) in gs and n + 1 < N:
                        group_start(units[n + 1]["qt"])
                    it_body(n)
                    for m in sched_minis.get(n, []):
                        m()
                DMA("sp", AT_d[p], atT.ap, [atT.b], [atb[p]], atT.b)

            late = {}

            def prefetch_ffn2():
                W2 = ffn_weights("2", w2g_d, w2u_d, w2d_d, with_wd=False)
                for ld in W2["loads"][:8]:
                    ld()
                late["W2"] = W2

            def prep_wo():
                R3 = 132 * KB
                wo = A.at("wo", R3, 8 * DM, BF16)
                wo3 = wo.ap.rearrange("p (kc n) -> p kc n", kc=8)
                gb2 = A.at("gb2", R3 + 16 * KB, 1024, F32)
                dtmp2 = A.at("dtmp2", ATT_OFF + 8 * KB, 256, F32, nbuf=2)
                assert ATT_OFF + 9 * KB <= CA_OFF
                DMA("pool", wo3, wo_d.rearrange("(kc p) n -> p kc n", p=128), (), [wo.b], wo.b)
                gate_bcast_tile(1, gb2.ap, gb2.b, dtmp2, dtmp2.bufs, [7, 7])
                for kc in range(8):
                    TT("pool", wo3[:, kc, :], wo3[:, kc, :], gb2.ap, ALU.mult, [wo.b, gb2.b], [wo.b])
                late["wo"] = wo
                late["wo3"] = wo3

            if UPFRONT:
                def stA(kb_):
                    sl = kb_ % 4
                    xb = xn_b.bufs[sl]
                    xap = xn_b.ap[:, sl * 1024:(sl + 1) * 1024]
                    hap = hn_b.ap[:, sl * 1024:(sl + 1) * 1024]
                    sb_ = ss_b.bufs[sl]
                    sc_ap = ss_b.ap[:, sl * 4:sl * 4 + 1]
                    ln_ap = ss_b.ap[:, sl * 4 + 1:sl * 4 + 2]
                    rs_ap = ss_b.ap[:, sl * 4 + 2:sl * 4 + 3]
                    DMA("sp", xap, X1_d[kb_ * 128:(kb_ + 1) * 128, :], [x1b[kb_ // 4]], [xb], xb)
                    MS("dve", sc_ap, 0.0, [sb_])
                    ACT(hap, xap, AF.Square, [xb, sb_], [hn_b.bufs[sl], sb_], accum=sc_ap)
                    ACT(ln_ap, sc_ap, AF.Ln, [sb_], [sb_], bias=epsT.ap[:, 0:1], scale=1.0 / DM)
                    ACT(rs_ap, ln_ap, AF.Exp, [sb_], [sb_], scale=-0.5)

                def stB(kb_):
                    sl = kb_ % 4
                    TS("dve", hn_b.ap[:, sl * 1024:(sl + 1) * 1024], xn_b.ap[:, sl * 1024:(sl + 1) * 1024], ss_b.ap[:, sl * 4 + 2:sl * 4 + 3],
                       None, ALU.mult, None, [xn_b.bufs[sl], ss_b.bufs[sl]], [hn_b.bufs[sl]])

                def stC(kb_):
                    sl = kb_ % 4
                    bk = 4 + kb_ % 4
                    pv = psb16(bk)
                    hap = hn_b.ap[:, sl * 1024:(sl + 1) * 1024]
                    for fc in range(8):
                        TR(pv[:, fc * 128:(fc + 1) * 128], hap[:, fc * 128:(fc + 1) * 128], ident_bf, [hn_b.bufs[sl], cbf.b], [PB[bk]])

                def stD(kb_):
                    bk = 4 + kb_ % 4
                    pv = psb16(bk)
                    i = kb_ // 4
                    for fc in range(8):
                        if fc % 4 == 3:
                            ACT(h2T3[:, fc, kb_ * 128:kb_ * 128 + 128], pv[:, fc * 128:(fc + 1) * 128], AF.Identity,
                                [PB[bk], gmul.b, modT.b], [h2T.bufs[i]],
                                bias=modT.ap[:, 24 + fc:24 + fc + 1], scale=gmul.ap[:, 8 + fc:8 + fc + 1])
                        else:
                            TS("dve", h2T3[:, fc, kb_ * 128:kb_ * 128 + 128], pv[:, fc * 128:(fc + 1) * 128],
                               gmul.ap[:, 8 + fc:8 + fc + 1], modT.ap[:, 24 + fc:24 + fc + 1], ALU.mult, ALU.add,
                               [PB[bk], gmul.b, modT.b], [h2T.bufs[i]])

                for k_ in range(32 + 3):
                    if k_ < 32:
                        stA(k_)
                    if 0 <= k_ - 1 < 32:
                        stB(k_ - 1)
                    if 0 <= k_ - 2 < 32:
                        stC(k_ - 2)
                    if 0 <= k_ - 3 < 32:
                        stD(k_ - 3)
                for i in range(8):
                    fproj_mini(i)()
                pair_setup(0)
                for m in qkv_minis(0, 0):
                    m()
                for m in lfc_minis():
                    m()
            else:
                for m in norm2_minis(0):
                    m()
                pair_setup(0)
                for m in qkv_minis(0, 0):
                    m()
                fproj_mini(0)()
            npairs = int(os.environ.get("NPAIRS", "8")) if stage >= 3 else 1
            for a in range(npairs):
                groups = []
                for qt in range(8):
                    ms = []
                    cur_fox = PAIR_ORDER[a] >= 4
                    mbanks = (4, 7) if not (os.environ.get('DUMMY_FOX') or os.environ.get('DUMMY_SB')) else (7,)
                    if qt < 7:
                        if a == 0 and not UPFRONT:
                            ms += norm2_minis(qt + 1)
                        ms += qkv_minis(a, qt + 1, mbanks)
                        if a == 0 and not UPFRONT:
                            ms.append(fproj_mini(qt + 1))
                    else:
                        if a == 0 and not UPFRONT:
                            ms += lfc_minis()
                        if a == 7 and stage >= 4:
                            ms.append(prefetch_ffn2)
                        if a + 1 < npairs:
                            ms.append(lambda a=a: pair_setup(a + 1))
                            for k_ in range(20):
                                def qm(a=a, k_=k_, mbanks=mbanks):
                                    if ("q0", a) not in late:
                                        late[("q0", a)] = qkv_minis(a + 1, 0, mbanks)
                                    lst = late[("q0", a)]
                                    if k_ < len(lst):
                                        lst[k_]()
                                ms.append(qm)
                    if a == 5 and qt == 3 and stage >= 4:
                        ms.append(prep_wo)
                    groups.append(ms)
                run_pair(a, groups)
            if dbg and stage < 4:
                DMA("sp", H2_d, h2T.ap, h2T.bufs, [], h2T.bufs[0])
                final_bufs.append(h2T.bufs[0])
            final_bufs.append(atT.b)

        if stage >= 4:
            W2 = late["W2"]
            ffn_weights_wd(W2)
            wo, wo3 = late["wo"], late["wo3"]
            R3 = 132 * KB
            att_t = A.at("att_t", R3 + 16 * KB, 2 * 4096, BF16, nbuf=2)
            xr3 = A.at("xr3", R3 + 32 * KB, 2 * 4096, F32, nbuf=2)
            def p3_views(i):
                sl = i % 2
                at3 = att_t.ap[:, sl * 4096:(sl + 1) * 4096].rearrange("p (kc t) -> p kc t", kc=8)
                x3 = xr3.ap[:, sl * 4096:(sl + 1) * 4096].rearrange("p (b n) -> p b n", b=4)
                return sl, at3, x3

            def p3_load(i):
                sl, at3, x3 = p3_views(i)
                DMA("sp", at3, AT_d.rearrange("pr p t -> p pr t")[:, :, i * 512:(i + 1) * 512], atb, [att_t.bufs[sl]], att_t.bufs[sl])
                DMA("sp", x3, X1_d[i * 512:(i + 1) * 512, :].rearrange("(b p) n -> p b n", p=128), [x1b[i]], [xr3.bufs[sl]], xr3.bufs[sl])

            p3_load(0)
            for i in range(8):
                sl, at3, x3 = p3_views(i)
                if i + 1 < 8:
                    p3_load(i + 1)
                for b in range(4):
                    for nh in range(2):
                        bk = (b * 2 + nh) % 4
                        for kc in range(8):
                            MM(psf(bk), at3[:, kc, b * 128:(b + 1) * 128], wo3[:, kc, nh * 512:(nh + 1) * 512], kc == 0, kc == 7,
                               [att_t.bufs[sl], wo.b], [PB[bk]])
                        TT("dve", x3[:, b, nh * 512:(nh + 1) * 512], psf(bk), x3[:, b, nh * 512:(nh + 1) * 512], ALU.add,
                           [PB[bk], xr3.bufs[sl]], [xr3.bufs[sl]])
                DMA("sp", X1_d[i * 512:(i + 1) * 512, :].rearrange("(b p) n -> p b n", p=128), x3, [xr3.bufs[sl]], [x1b[i]], xr3.bufs[sl])
            for ld in W2["loads"][8:]:
                ld()
            gb3 = A.at("gb3", XRES_OFF, 1024, F32)
            dtmp3 = A.at("dtmp3", XRES_OFF + 4 * KB, 256, F32, nbuf=2)
            gate_bcast_tile(2, gb3.ap, gb3.b, dtmp3, dtmp3.bufs, [2, 3])
            fold_gate(W2, 2, gb3.ap, gb3.b)
            sg2 = A.at("sg2", SG_OFF, 1024, BF16, nbuf=2)
            if stage >= 5:
                xr = ffn_phase("2", X1_d, x1b, out_d, None, W2, 2, sg2)
                final_bufs = [xr.b]
            else:
                final_bufs += [xr3.bufs[0], xr3.bufs[1]]

        fin = S.op("sp", None, writes=final_bufs) if final_bufs else None
        S.emit()
    return nc


_CACHE = {}


def _inputs_for_core(inp, b, cbf, cf32):
    m = {
        "x": np.ascontiguousarray(inp["x"][b]),
        "c": np.ascontiguousarray(inp["c"][b:b + 1]),
        "w_mod": inp["w_mod"][0], "b_mod": inp["b_mod"][0:1],
        "g_ffn1": inp["g_ffn1"][0:1], "w1_gate": inp["w1_gate"][0], "w1_up": inp["w1_up"][0], "w1_down": inp["w1_down"][0],
        "g_mix": inp["g_mix"][0:1], "w_in": inp["w_in"][0], "b_f": inp["b_f"][0:1],
        "g_q": inp["g_q"][0].reshape(1, 512), "g_k": inp["g_k"][0].reshape(1, 512),
        "w_o": inp["w_o"][0], "g_ffn2": inp["g_ffn2"][0:1],
        "w2_gate": inp["w2_gate"][0], "w2_up": inp["w2_up"][0], "w2_down": inp["w2_down"][0],
        "cbf": cbf, "cf32": cf32,
    }
    return {k: np.ascontiguousarray(np.asarray(v, dtype=np.float32)) for k, v in m.items()}


def kernel(**inputs):
    cbf, cf32 = _consts()
    nc = build()
    in_maps = [_inputs_for_core(inputs, b, cbf, cf32) for b in range(8)]
    res = run_bass_kernel_spmd(nc, in_maps, core_ids=list(range(8)))
    return np.stack([np.asarray(r["out"], dtype=np.float32) for r in res.results], axis=0)
```

```python
from contextlib import ExitStack
import os
import numpy as np
import concourse.bass as bass
import concourse.mybir as mybir
from concourse.bass_utils import run_bass_kernel_spmd

F32 = mybir.dt.float32
BF16 = mybir.dt.bfloat16
AF = mybir.ActivationFunctionType
ALU = mybir.AluOpType

SEG = 30000
EMBED_WAIT = os.environ.get("EMBED_WAIT", "1") == "1"
S_TOK = 4096
DM = 1024
DFF = 2816
NJ = 22
NEG = -30000.0
EPS = 1e-6
ARENA_BYTES = 212800


class Buf:
    __slots__ = ("name", "last_w", "readers", "dcount", "excl")

    def __init__(self, name, excl=False):
        self.name = name
        self.last_w = None
        self.readers = {}
        self.dcount = 0
        self.excl = excl

    def pending(self):
        ops = list(self.readers.values())
        if self.last_w is not None:
            ops.append(self.last_w)
        return ops


class Op:
    __slots__ = ("eng", "fn", "deps", "flag", "is_dma", "anchor", "val", "gidx")


class Sched:
    ENGS = ["pe", "act", "dve", "pool", "sp"]

    def __init__(self, nc):
        self.nc = nc
        self.ops = {e: [] for e in self.ENGS}
        self.anchors = []
        self.nops = 0

    def op(self, eng, fn, reads=(), writes=(), dma=None):
        o = Op()
        o.eng = eng
        o.fn = fn
        o.flag = False
        o.is_dma = dma is not None
        o.anchor = dma
        o.val = None
        o.gidx = self.nops
        self.nops += 1
        deps = {}

        def add(d, kind):
            if d is None:
                return
            if not o.is_dma and not d.is_dma and d.eng == eng:
                if eng == "pe" or kind == "rar":
                    return
            if d.is_dma and o.is_dma and d.anchor is o.anchor and d.eng == eng and kind == "waw":
                return
            deps[id(d)] = d

        for b in reads:
            add(b.last_w, "raw")
            if b.excl:
                for r in b.readers.values():
                    add(r, "rar")
        for b in writes:
            add(b.last_w, "waw")
            for r in b.readers.values():
                add(r, "war")
        o.deps = list(deps.values())
        for d in o.deps:
            d.flag = True
        for b in reads:
            key = ("dma", o.gidx) if o.is_dma else eng
            b.readers[key] = o
        for b in writes:
            b.last_w = o
            b.readers = {}
        if o.is_dma:
            if dma not in self.anchors:
                self.anchors.append(dma)
            dma.dcount += 16
            o.val = dma.dcount
            o.flag = True
        self.ops[eng].append(o)
        return o

    def emit(self):
        nc = self.nc
        nseg = {}
        for e in self.ENGS:
            c = 0
            for o in self.ops[e]:
                if not o.is_dma and o.flag:
                    c += 1
                    o.val = c
            nseg[e] = (c + SEG - 1) // SEG if c else 0
        with ExitStack() as st:
            esem = {e: [st.enter_context(nc.semaphore(f"s_{e}{i}")) for i in range(nseg[e])] for e in self.ENGS}
            dsem = {id(a): st.enter_context(nc.semaphore(f"d{k}")) for k, a in enumerate(self.anchors)}
            block = st.enter_context(nc.Block())

            def semval(d):
                if d.is_dma:
                    return dsem[id(d.anchor)], d.val, ("d", id(d.anchor))
                s = (d.val - 1) // SEG
                return esem[d.eng][s], d.val - s * SEG, ("e", d.eng)

            def run(e, engine):
                waited = {}
                for o in self.ops[e]:
                    need = {}
                    for d in o.deps:
                        _, _, key = semval(d)
                        if waited.get(key, 0) >= d.val:
                            continue
                        if key not in need or need[key].val < d.val:
                            need[key] = d
                    items = list(need.items())
                    embed = None
                    if EMBED_WAIT and o.fn is not None and not o.is_dma and items:
                        embed = items.pop()
                    for key, d in items:
                        s, v, _ = semval(d)
                        engine.wait_ge(s, v)
                        waited[key] = d.val
                    if o.fn is None:
                        continue
                    ins = o.fn(engine)
                    if embed is not None:
                        s, v, _ = semval(embed[1])
                        ins._wait_ge(s, v)
                        waited[embed[0]] = embed[1].val
                    if o.is_dma:
                        ins.then_inc(dsem[id(o.anchor)], 16)
                    elif o.flag:
                        s, _, _ = semval(o)
                        ins.then_inc(s, 1)

            block.tensor(lambda eng: run("pe", eng))
            block.scalar(lambda eng: run("act", eng))
            block.vector(lambda eng: run("dve", eng))
            block.gpsimd(lambda eng: run("pool", eng))
            block.sync(lambda eng: run("sp", eng))


class Tile:
    __slots__ = ("ap", "off", "size", "bufs", "name")

    @property
    def b(self):
        return self.bufs[0]


class Arena:
    def __init__(self, nc, st, nbytes):
        self.t = st.enter_context(nc.sbuf_tensor("arena", [128, nbytes // 2], BF16))
        self.nbytes = nbytes
        self.tiles = []

    def at(self, name, off, cols, dt, nbuf=1):
        esz = 4 if dt == F32 else 2
        size = cols * esz
        assert off % 4 == 0 and off + size <= self.nbytes, (name, off, size)
        ap = self.t[:, off // 2: off // 2 + size // 2]
        if dt == F32:
            ap = ap.bitcast(F32)
        t = Tile()
        t.name = name
        t.ap = ap
        t.off = off
        t.size = size
        t.bufs = [Buf(f"{name}{i}") for i in range(nbuf)]
        inh = {}
        for o in self.tiles:
            if o.off < off + size and off < o.off + o.size:
                for b in o.bufs:
                    for p in b.pending():
                        inh[id(p)] = p
        for b in t.bufs:
            for k, p in inh.items():
                b.readers[("inh", k)] = p
        self.tiles.append(t)
        return t


def _consts():
    idn = np.eye(128, dtype=np.float32)
    j = np.arange(128)[:, None]
    s = np.arange(128)[None, :]
    negU = np.where(j >= s, -1.0, 0.0).astype(np.float32)
    blk = np.where((j // 64) == (s // 64), 1.0 / 64.0, 0.0).astype(np.float32)
    diagS = np.where(j < s, 0.0, NEG).astype(np.float32)
    diagF = np.where(j <= s, 0.0, NEG).astype(np.float32)
    negcol = np.full((128, 384), NEG, np.float32)
    zer = np.zeros((128, 384), np.float32)
    mS = np.concatenate([negcol, diagS, zer], 1)
    mF = np.concatenate([negcol, diagF, zer], 1)
    cbf = np.concatenate([idn, negU, blk, mS, mF], 1)
    triL = np.where(j <= s, -1.0, 0.0).astype(np.float32)
    sel = np.zeros((128, 8, 128), np.float32)
    for h in range(8):
        sel[h, h, 64] = 1.0
        sel[32 + h, h, 96] = 1.0
    cf32 = np.concatenate([idn, triL, sel.reshape(128, 1024)], 1)
    return np.ascontiguousarray(cbf), np.ascontiguousarray(cf32)


PAIR_ORDER = [0, 4, 5, 6, 7, 1, 2, 3]


def build(stage=99, dbg=False):
    nc = bass.Bass("TRN2", target_bir_lowering=False)

    def din(name, shape, dt=F32):
        return nc.dram_tensor(name, shape, dt, kind="ExternalInput").ap()

    x_d = din("x", [S_TOK, DM])
    c_d = din("c", [1, DM])
    wmod_d = din("w_mod", [DM, 9 * DM])
    bmod_d = din("b_mod", [1, 9 * DM])
    g1_d = din("g_ffn1", [1, DM])
    w1g_d = din("w1_gate", [DM, DFF])
    w1u_d = din("w1_up", [DM, DFF])
    w1d_d = din("w1_down", [DFF, DM])
    gm_d = din("g_mix", [1, DM])
    win_d = din("w_in", [DM, 3080])
    bf_d = din("b_f", [1, 8])
    gq_d = din("g_q", [1, 512])
    gk_d = din("g_k", [1, 512])
    wo_d = din("w_o", [DM, DM])
    g2_d = din("g_ffn2", [1, DM])
    w2g_d = din("w2_gate", [DM, DFF])
    w2u_d = din("w2_up", [DM, DFF])
    w2d_d = din("w2_down", [DFF, DM])
    cbf_d = din("cbf", [128, 2176])
    cf32_d = din("cf32", [128, 1280])
    out_d = nc.dram_tensor("out", [S_TOK, DM], F32, kind="ExternalOutput").ap()
    skind = "ExternalOutput" if dbg else "Internal"
    X1_d = nc.dram_tensor("X1s", [S_TOK, DM], F32, kind=skind).ap()
    AT_d = nc.dram_tensor("ATs", [8, 128, S_TOK], BF16, kind=skind).ap()
    if dbg:
        H2_d = nc.dram_tensor("H2s", [128, 8 * S_TOK], BF16, kind="ExternalOutput").ap()
        MOD_d = nc.dram_tensor("MODs", [128, 72], F32, kind="ExternalOutput").ap()

    S = Sched(nc)
    with ExitStack() as st:
        A = Arena(nc, st, ARENA_BYTES)
        PS = [st.enter_context(nc.psum_tensor(f"ps{i}", [128, 512], F32)) for i in range(8)]
        PB = [Buf(f"ps{i}", excl=True) for i in range(8)]

        def psf(i):
            return PS[i][:, :]

        def psb16(i):
            return PS[i][:, :].bitcast(BF16)

        def MM(out, lhsT, rhs, start, stop, reads, writes, nochk=False):
            if nochk:
                S.op("pe", lambda e: e.matmul(out, lhsT=lhsT, rhs=rhs, start=start, stop=stop, skip_group_check=True), reads, writes)
            else:
                S.op("pe", lambda e: e.matmul(out, lhsT=lhsT, rhs=rhs, start=start, stop=stop), reads, writes)

        def TR(out, in_, ident, reads, writes):
            S.op("pe", lambda e: e.transpose(out, in_, ident), reads, writes)

        def ACT(out, in_, func, reads, writes, bias=None, scale=None, accum=None):
            kw = {}
            if bias is not None:
                kw["bias"] = bias
            if scale is not None:
                kw["scale"] = scale
            if accum is not None:
                kw["accum_out"] = accum
            S.op("act", lambda e: e.activation(out=out, in_=in_, func=func, **kw), reads, writes)

        def TT(eng, out, in0, in1, op, reads, writes):
            S.op(eng, lambda e: e.tensor_tensor(out=out, in0=in0, in1=in1, op=op), reads, writes)

        def TS(eng, out, in0, s1, s2, op0, op1, reads, writes):
            if s2 is None:
                S.op(eng, lambda e: e.tensor_scalar(out=out, in0=in0, scalar1=s1, scalar2=None, op0=op0), reads, writes)
            else:
                S.op(eng, lambda e: e.tensor_scalar(out=out, in0=in0, scalar1=s1, scalar2=s2, op0=op0, op1=op1), reads, writes)

        def STT(eng, out, in0, scalar, in1, op0, op1, reads, writes):
            S.op(eng, lambda e: e.scalar_tensor_tensor(out=out, in0=in0, scalar=scalar, in1=in1, op0=op0, op1=op1), reads, writes)

        def CP(eng, out, in_, reads, writes):
            S.op(eng, lambda e: e.tensor_copy(out=out, in_=in_), reads, writes)

        def MS(eng, ap, val, writes):
            S.op(eng, lambda e: e.memset(ap, val), (), writes)

        def DMA(q, out, in_, reads, writes, anchor, slow=False):
            if slow:
                S.op(q, lambda e: e.dma_start(out=out, in_=in_, allow_slow_non_contiguous=True), reads, writes, dma=anchor)
            else:
                S.op(q, lambda e: e.dma_start(out=out, in_=in_), reads, writes, dma=anchor)

        top = ARENA_BYTES
        def ptile(name, cols, dt, nbuf=1):
            nonlocal top
            esz = 4 if dt == F32 else 2
            top -= ((cols * esz + 31) // 32) * 32
            return A.at(name, top, cols, dt, nbuf)

        cbf = ptile("cbf", 128, BF16)
        ident_bf = cbf.ap[:, 0:128]
        ones32 = ptile("ones32", 128, F32)
        cf_i = ptile("cf_i", 256, F32)
        ident32 = cf_i.ap[:, 0:128]
        triL32 = cf_i.ap[:, 128:256]
        modT = ptile("modT", 72, F32)
        bmodT = ptile("bmodT", 72, F32)
        gvec = ptile("gvec", 24, F32)
        cT = ptile("cT", 8, F32)
        cactT = ptile("cactT", 8, BF16)
        gmul = ptile("gmul", 24, F32)
        gate = ptile("gate", 24, F32)
        gqk = ptile("gqk", 8, F32)
        bfB = ptile("bfB", 8, F32)
        ss = ptile("ss", 8, F32, nbuf=2)
        epsT = ptile("epsT", 8, F32)
        MS("dve", epsT.ap, EPS, [epsT.b])
        PERS = top

        DMA("pool", cbf.ap, cbf_d[:, 0:128], (), [cbf.b], cbf.b)
        DMA("sp", cf_i.ap, cf32_d[:, 0:256], (), [cf_i.b], cf_i.b)
        MS("dve", ones32.ap, 1.0, [ones32.b])
        DMA("sp", bmodT.ap, bmod_d.rearrange("o (c p) -> p (o c)", p=128), (), [bmodT.b], bmodT.b, slow=True)
        DMA("sp", cT.ap, c_d.rearrange("o (c p) -> p (o c)", p=128), (), [cT.b], cT.b, slow=True)
        for k, gd in enumerate([g1_d, gm_d, g2_d]):
            DMA("sp", gvec.ap[:, k * 8:(k + 1) * 8], gd.rearrange("o (c p) -> p (o c)", p=128), (), [gvec.b], gvec.b, slow=True)
        DMA("sp", gqk.ap[:, 0:4], gq_d.rearrange("o (c p) -> p (o c)", p=128), (), [gqk.b], gqk.b, slow=True)
        DMA("sp", gqk.ap[:, 4:8], gk_d.rearrange("o (c p) -> p (o c)", p=128), (), [gqk.b], gqk.b, slow=True)
        DMA("sp", bfB.ap, bf_d.partition_broadcast(128), (), [bfB.b], bfB.b)
        TS("dve", gqk.ap[:, 0:4], gqk.ap[:, 0:4], 0.125, None, ALU.mult, None, [gqk.b], [gqk.b])
        ACT(cactT.ap, cT.ap, AF.Silu, [cT.b], [cactT.b])

        KB = 1024
        W_OFF = 0
        R_OFF = 132 * KB
        XRES_OFF = R_OFF
        XN_OFF = XRES_OFF + 16 * KB
        HN_OFF = XN_OFF + 8 * KB
        HT_OFF = HN_OFF + 8 * KB
        AT_OFF = HT_OFF + 8 * KB
        SG_OFF = AT_OFF + 22 * KB
        R_END = SG_OFF + 2 * KB
        assert R_END <= PERS, (R_END, PERS)

        WM_OFF = 196 * KB
        assert WM_OFF + 8 * KB <= PERS, PERS
        wm = A.at("wm", WM_OFF, 4096, BF16, nbuf=2)
        wmod_v = wmod_d.rearrange("(kc p) n -> p kc n", p=128)

        def mod_group(g, bank=7):
            buf = wm.bufs[g % 2]
            wap = wm.ap[:, (g % 2) * 2048:(g % 2 + 1) * 2048].rearrange("p (kc n) -> p kc n", kc=8)
            DMA("pool", wap, wmod_v[:, :, g * 256:(g + 1) * 256], (), [buf], buf)
            for m in range(2):
                col = g * 2 + m
                for kc in range(8):
                    MM(PS[bank][:, col:col + 1], wap[:, kc, m * 128:(m + 1) * 128], cactT.ap[:, kc:kc + 1],
                       kc == 0, kc == 7, [buf, cactT.b], [PB[bank]])
            TT("dve", modT.ap[:, g * 2:g * 2 + 2], PS[bank][:, g * 2:g * 2 + 2], bmodT.ap[:, g * 2:g * 2 + 2], ALU.add,
               [PB[bank], bmodT.b], [modT.b])

        def mod_derive(k):
            sc = modT.ap[:, 24 * k + 8:24 * k + 16]
            STT("dve", gmul.ap[:, 8 * k:8 * k + 8], sc, 1.0, gvec.ap[:, 8 * k:8 * k + 8], ALU.add, ALU.mult,
                [modT.b, gvec.b], [gmul.b])

        def gate_derive(k):
            ga = modT.ap[:, 24 * k + 16:24 * k + 24]
            if k == 1:
                TS("dve", gate.ap[:, 8:16], ga, 1.0, None, ALU.add, None, [modT.b], [gate.b])
            else:
                TS("dve", gate.ap[:, 8 * k:8 * k + 8], ga, 0.5, 0.5, ALU.mult, ALU.add, [modT.b], [gate.b])

        def gate_bcast_tile(k, dst, dst_buf, tmp, tmp_buf, banks):
            for hh in range(2):
                bk = banks[hh]
                for c4 in range(4):
                    c = hh * 4 + c4
                    TS("dve", tmp.ap[:, (c % 2) * 128:(c % 2 + 1) * 128], ident32, gate.ap[:, 8 * k + c:8 * k + c + 1], None,
                       ALU.mult, None, [cf_i.b, gate.b], [tmp_buf[c % 2]])
                    MM(PS[bk][:, c4 * 128:(c4 + 1) * 128], ones32.ap, tmp.ap[:, (c % 2) * 128:(c % 2 + 1) * 128],
                       True, True, [ones32.b, tmp_buf[c % 2]], [PB[bk]])
                CP("dve", dst[:, hh * 512:(hh + 1) * 512], psf(bk), [PB[bk]], [dst_buf])

        for g in range(12):
            mod_group(g)
        mod_derive(0)
        gate_derive(0)

        JG = [(0, 6), (6, 12), (12, 17), (17, 22)]

        def ffn_weights(tag, wg_d, wu_d, wd_d, with_wd=True):
            Wg = A.at(f"Wg{tag}", W_OFF, 8 * DFF, BF16, nbuf=4)
            Wu = A.at(f"Wu{tag}", W_OFF + 44 * KB, 8 * DFF, BF16, nbuf=4)
            Wg3 = Wg.ap.rearrange("p (kc n) -> p kc n", kc=8)
            Wu3 = Wu.ap.rearrange("p (kc n) -> p kc n", kc=8)
            wgv = wg_d.rearrange("(kc p) n -> p kc n", p=128)
            wuv = wu_d.rearrange("(kc p) n -> p kc n", p=128)
            loads = []
            for gi, (j0, j1) in enumerate(JG):
                loads.append(lambda gi=gi, j0=j0, j1=j1: DMA("pool", Wg3[:, :, j0 * 128:j1 * 128], wgv[:, :, j0 * 128:j1 * 128], (), [Wg.bufs[gi]], Wg.bufs[gi]))
                loads.append(lambda gi=gi, j0=j0, j1=j1: DMA("pool", Wu3[:, :, j0 * 128:j1 * 128], wuv[:, :, j0 * 128:j1 * 128], (), [Wu.bufs[gi]], Wu.bufs[gi]))
            W = dict(Wg=Wg, Wu=Wu, Wg3=Wg3, Wu3=Wu3, loads=loads, tag=tag, wd_d=wd_d)
            if with_wd:
                ffn_weights_wd(W)
            return W

        def ffn_weights_wd(W):
            Wd = A.at(f"Wd{W['tag']}", W_OFF + 88 * KB, NJ * DM, BF16, nbuf=2)
            Wd3 = Wd.ap.rearrange("p (j n) -> p j n", j=NJ)
            wdv = W["wd_d"].rearrange("(j p) n -> p j n", p=128)
            for hh in range(2):
                W["loads"].append(lambda hh=hh: DMA("pool", Wd3[:, hh * 11:(hh + 1) * 11, :], wdv[:, hh * 11:(hh + 1) * 11, :], (), [Wd.bufs[hh]], Wd.bufs[hh]))
            W["Wd"] = Wd
            W["Wd3"] = Wd3

        def fold_gate(W, k, gb_ap, gb_buf):
            for hh in range(2):
                for q in range(11):
                    j = hh * 11 + q
                    TT("pool", W["Wd3"][:, j, :], W["Wd3"][:, j, :], gb_ap, ALU.mult, [W["Wd"].bufs[hh], gb_buf], [W["Wd"].bufs[hh]])

        def jgroup(j):
            for gi, (j0, j1) in enumerate(JG):
                if j0 <= j < j1:
                    return gi

        tr_ctr = [0]

        def transpose_block(hn_ap, hn_buf, k, dstT3, dst_buf, tcol0, banks=(4, 5)):
            bk = banks[tr_ctr[0] % len(banks)]
            tr_ctr[0] += 1
            pv = psb16(bk)
            for fc in range(8):
                TR(pv[:, fc * 128:(fc + 1) * 128], hn_ap[:, fc * 128:(fc + 1) * 128], ident_bf, [hn_buf, cbf.b], [PB[bk]])
            for fc in range(8):
                TS("dve", dstT3[:, fc, tcol0:tcol0 + 128], pv[:, fc * 128:(fc + 1) * 128],
                   gmul.ap[:, 8 * k + fc:8 * k + fc + 1], modT.ap[:, 24 * k + fc:24 * k + fc + 1], ALU.mult, ALU.add,
                   [PB[bk], gmul.b, modT.b], [dst_buf])

        nb_ctr = [0]

        def norm_block(src_rows, src_bufs, xn_t, hap, hb):
            slot = nb_ctr[0] % 2
            nb_ctr[0] += 1
            xb = xn_t.bufs[slot]
            xap = xn_t.ap[:, slot * 1024:(slot + 1) * 1024]
            DMA("sp", xap, src_rows, src_bufs, [xb], xb)
            sb_ = ss.bufs[slot]
            sc_ap = ss.ap[:, slot * 4:slot * 4 + 1]
            ln_ap = ss.ap[:, slot * 4 + 1:slot * 4 + 2]
            rs_ap = ss.ap[:, slot * 4 + 2:slot * 4 + 3]
            MS("dve", sc_ap, 0.0, [sb_])
            ACT(hap, xap, AF.Square, [xb, sb_], [hb, sb_], accum=sc_ap)
            ACT(ln_ap, sc_ap, AF.Ln, [sb_], [sb_], bias=epsT.ap[:, 0:1], scale=1.0 / DM)
            ACT(rs_ap, ln_ap, AF.Exp, [sb_], [sb_], scale=-0.5)
            TS("dve", hap, xap, rs_ap, None, ALU.mult, None, [xb, sb_], [hb])

        def ffn_phase(tag, src_d, src_bufs, dst_d, dst_bufs, W, k, sg, inter=None):
            xn = A.at(f"xn{tag}", XN_OFF, 2048, F32, nbuf=2)
            hn = A.at(f"hn{tag}", HN_OFF, 4096, BF16, nbuf=4)
            hT = A.at(f"hT{tag}", HT_OFF, 4096, BF16)
            hT3 = hT.ap.rearrange("p (kc t) -> p kc t", kc=8)
            hn3 = hn.ap.rearrange("p (b n) -> p b n", b=4)
            ATt = A.at(f"AT{tag}", AT_OFF, NJ * 512, BF16, nbuf=NJ)
            AT3 = ATt.ap.rearrange("p (j t) -> p j t", j=NJ)
            xres = None

            def norm_part(i, b):
                blk = i * 4 + b
                norm_block(src_d[blk * 128:(blk + 1) * 128, :], [src_bufs[i]] if src_bufs else [], xn, hn3[:, b, :], hn.bufs[b])

            def tr_part(i, b):
                transpose_block(hn3[:, b, :], hn.bufs[b], k, hT3, hT.b, b * 128)

            EARLY = os.environ.get("EARLYNORM", "1") == "1"
            for b in range(4):
                norm_part(0, b)
                tr_part(0, b)
            for i in range(8):
                for j in range(NJ):
                    gi = jgroup(j)
                    for kc in range(8):
                        MM(psf(j % 2), W["Wg3"][:, kc, j * 128:(j + 1) * 128], hT3[:, kc, :], kc == 0, kc == 7,
                           [W["Wg"].bufs[gi], hT.b], [PB[j % 2]])
                    for kc in range(8):
                        MM(psf(2 + j % 2), W["Wu3"][:, kc, j * 128:(j + 1) * 128], hT3[:, kc, :], kc == 0, kc == 7,
                           [W["Wu"].bufs[gi], hT.b], [PB[2 + j % 2]])
                    sgap = sg.ap[:, (j % 2) * 512:(j % 2 + 1) * 512]
                    ACT(sgap, psf(j % 2), AF.Silu, [PB[j % 2]], [sg.bufs[j % 2]])
                    TT("dve", AT3[:, j, :], psf(2 + j % 2), sgap, ALU.mult, [PB[2 + j % 2], sg.bufs[j % 2]], [ATt.bufs[j]])
                    if inter is not None:
                        inter(i, j)
                    if EARLY and i + 1 < 8 and j in (3, 8, 13, 18):
                        norm_part(i + 1, (j - 3) // 5)
                if i + 1 < 8:
                    for b in range(4):
                        if not EARLY:
                            norm_part(i + 1, b)
                        tr_part(i + 1, b)
                if xres is None:
                    xres = A.at(f"xres{tag}", XRES_OFF, 4096, F32)
                    xres3 = xres.ap.rearrange("p (b n) -> p b n", b=4)
                DMA("sp", xres3, src_d[i * 512:(i + 1) * 512, :].rearrange("(b p) n -> p b n", p=128),
                    [src_bufs[i]] if src_bufs else [], [xres.b], xres.b)
                for b in range(4):
                    for nh in range(2):
                        bk = 6 + (b * 2 + nh) % 2
                        for j in range(NJ):
                            MM(psf(bk), AT3[:, j, b * 128:(b + 1) * 128], W["Wd3"][:, j, nh * 512:(nh + 1) * 512], j == 0, j == NJ - 1,
                               [ATt.bufs[j], W["Wd"].bufs[j // 11]], [PB[bk]])
                        TT("dve", xres3[:, b, nh * 512:(nh + 1) * 512], psf(bk), xres3[:, b, nh * 512:(nh + 1) * 512], ALU.add,
                           [PB[bk], xres.b], [xres.b])
                DMA("sp", dst_d[i * 512:(i + 1) * 512, :].rearrange("(b p) n -> p b n", p=128), xres3, [xres.b],
                    [dst_bufs[i]] if dst_bufs else [], xres.b)
            return xres

        W1 = ffn_weights("1", w1g_d, w1u_d, w1d_d)
        for ld in W1["loads"]:
            ld()
        gb1 = A.at("gb1", XRES_OFF, 1024, F32)
        dtmp = A.at("dtmp", XRES_OFF + 4 * KB, 256, F32, nbuf=2)
        gate_bcast_tile(0, gb1.ap, gb1.b, dtmp, dtmp.bufs, [2, 3])
        fold_gate(W1, 0, gb1.ap, gb1.b)
        sg1 = A.at("sg1", SG_OFF, 1024, BF16, nbuf=2)

        pending_groups = list(range(12, 36))

        def inter1(i, j):
            if i in (1, 2) and j % 2 == 0 and pending_groups:
                mod_group(pending_groups.pop(0))

        x1b = [Buf(f"X1d{i}") for i in range(8)]
        final_bufs = []
        if stage >= 1 and not os.environ.get('SKIPFFN1'):
            xr = ffn_phase("1", x_d, None, X1_d, x1b, W1, 0, sg1, inter=inter1)
            if stage == 1:
                final_bufs.append(xr.b)
        while pending_groups:
            mod_group(pending_groups.pop(0))
        mod_derive(1)
        mod_derive(2)
        gate_derive(1)
        gate_derive(2)
        if dbg:
            DMA("sp", MOD_d, modT.ap, [modT.b], [], modT.b)
            final_bufs.append(modT.b)

        atb = [Buf(f"ATd{i}") for i in range(8)]
        if stage >= 2:
            SLOT_OFF = [64 * KB, 89 * KB]
            QT_OFF = 114 * KB
            WC_OFF = 118 * KB
            WF_OFF = 130 * KB
            FT_OFF = 131 * KB
            DF_OFF = 134 * KB
            BQ_OFF = 142 * KB
            CR_OFF = 144 * KB
            TMP_OFF = 152 * KB
            ATT_OFF = 184 * KB
            assert ATT_OFF + 8 * KB <= PERS, (ATT_OFF, PERS)
            win_v = win_d.rearrange("(kc p) n -> p kc n", p=128)
            CA_OFF = 193 * KB
            cbfA = A.at("cbfA", CA_OFF, 2048, BF16)
            negU_bf = cbfA.ap[:, 0:128]
            blk_bf = cbfA.ap[:, 128:256]
            mS_bf = cbfA.ap[:, 256:1152]
            mF_bf = cbfA.ap[:, 1152:2048]
            cbf2 = A.at("cbf2", CA_OFF + 4 * KB, 1024, BF16)
            sel_bf = cbf2.ap
            negones = A.at("negones", CA_OFF + 6 * KB, 128, BF16)
            assert CA_OFF + 6 * KB + 256 <= PERS, PERS
            DMA("pool", cbfA.ap, cbf_d[:, 128:2176], (), [cbfA.b], cbfA.b)
            DMA("pool", cbf2.ap, cf32_d[:, 256:1280], (), [cbf2.b], cbf2.b)
            MS("pool", negones.ap, -1.0, [negones.b])

            h2T = A.at("h2T", 0, 8 * S_TOK, BF16, nbuf=8)
            h2T3 = h2T.ap.rearrange("p (kc t) -> p kc t", kc=8)
            xn_b = A.at("xn_b", SLOT_OFF[1], 4096, F32, nbuf=4)
            hn_b = A.at("hn_b", SLOT_OFF[1] + 16 * KB, 4096, BF16, nbuf=4)
            ss_b = A.at("ss_b", SLOT_OFF[1] + 24 * KB, 16, F32, nbuf=4)
            qtiles = [[A.at(f"q{s_}{x_}", QT_OFF + s_ * 2 * KB + x_ * KB, 512, BF16) for x_ in range(2)] for s_ in range(2)]
            wcs = [A.at(f"wc{s_}", WC_OFF + s_ * 6 * KB, 3 * 1024, BF16, nbuf=3) for s_ in range(2)]
            wf = A.at("wf", WF_OFF, 64, BF16)
            fT = A.at("fT", FT_OFF, 256, F32)
            spT = A.at("spT", FT_OFF + KB, 256, F32)
            lfcT = A.at("lfcT", FT_OFF + 2 * KB, 256, F32)
            dF = A.at("dF", DF_OFF, 4096, BF16, nbuf=8)
            biasq = A.at("biasq", BQ_OFF, 512, F32, nbuf=2)
            crefB = A.at("crefB", CR_OFF, 64, F32)
            totT = A.at("totT", CR_OFF + 256, 256, F32)
            offsT = A.at("offsT", CR_OFF + 256 + KB, 256, F32)
            r32 = A.at("r32", CR_OFF + 256 + 2 * KB, 512, F32)
            d32 = A.at("d32", CR_OFF + 256 + 4 * KB, 512, F32)
            hib = A.at("hib", CR_OFF + 256 + 6 * KB, 512, BF16)
            t_ = TMP_OFF
            Esb = A.at("Esb", t_, 1024, F32, nbuf=2); t_ += 4 * KB
            SPsb = A.at("SPsb", t_, 1024, BF16, nbuf=2); t_ += 2 * KB
            tmps = A.at("tmps", t_, 1024, F32, nbuf=2); t_ += 4 * KB
            wsb = A.at("wsb", t_, 1024, BF16, nbuf=2); t_ += 2 * KB
            Rsb = A.at("Rsb", t_, 1024, F32, nbuf=2); t_ += 4 * KB
            Psb = A.at("Psb", t_, 1536, BF16, nbuf=3); t_ += 3 * KB
            osb = A.at("osb", t_, 1024, F32, nbuf=2); t_ += 4 * KB
            rbs = A.at("rbs", t_, 1024, F32, nbuf=2); t_ += 4 * KB
            q32 = A.at("q32", t_, 512, F32); t_ += 2 * KB
            sqb = A.at("sqb", t_, 512, BF16); t_ += 1 * KB
            rsq = A.at("rsq", t_, 512, F32); t_ += 2 * KB
            assert t_ <= ATT_OFF, (t_, ATT_OFF)
            atT = A.at("atT", ATT_OFF, S_TOK, BF16)
            fT3 = fT.ap.rearrange("p (b h) -> p b h", h=8)
            spT3 = spT.ap.rearrange("p (b h) -> p b h", h=8)
            lfcT3 = lfcT.ap.rearrange("p (b h) -> p b h", h=8)
            totT3 = totT.ap.rearrange("p (b h) -> p b h", h=8)
            offsT3 = offsT.ap.rearrange("p (b h) -> p b h", h=8)
            crefB3 = crefB.ap.rearrange("p (q h) -> p q h", h=8)

            DMA("pool", wf.ap.rearrange("p (kc n) -> p kc n", kc=8), win_v[:, :, 3072:3080], (), [wf.b], wf.b, slow=True)
            wf3 = wf.ap.rearrange("p (kc n) -> p kc n", kc=8)
            MS("pool", dF.ap[0:64, :], 0.0, dF.bufs)

            nb2 = [0]

            n2bank = [0]
            UPFRONT = os.environ.get("UPFRONT", "1") == "1"

            def norm2_minis(i):
                out = []
                for b in range(4):
                    st_ = {}
                    def f1(b=b, st_=st_):
                        blk = i * 4 + b
                        slot = nb2[0] % 2
                        nb2[0] += 1
                        st_["hb"] = hn_b.bufs[slot]
                        st_["hap"] = hn_b.ap[:, slot * 1024:(slot + 1) * 1024]
                        norm_block(X1_d[blk * 128:(blk + 1) * 128, :], [x1b[i]], xn_b, st_["hap"], st_["hb"])
                    def f2(b=b, st_=st_):
                        n2bank[0] += 1
                        st_["bk"] = (4 + n2bank[0] % 4) if UPFRONT else 7
                        pv = psb16(st_["bk"])
                        for fc in range(8):
                            TR(pv[:, fc * 128:(fc + 1) * 128], st_["hap"][:, fc * 128:(fc + 1) * 128], ident_bf, [st_["hb"], cbf.b], [PB[st_["bk"]]])
                    def f3(b=b, st_=st_):
                        blk = i * 4 + b
                        pv = psb16(st_["bk"])
                        for fc in range(8):
                            if UPFRONT and fc % 2 == 1:
                                ACT(h2T3[:, fc, blk * 128:blk * 128 + 128], pv[:, fc * 128:(fc + 1) * 128], AF.Identity,
                                    [PB[st_["bk"]], gmul.b, modT.b], [h2T.bufs[i]],
                                    bias=modT.ap[:, 24 + fc:24 + fc + 1], scale=gmul.ap[:, 8 + fc:8 + fc + 1])
                            else:
                                TS("dve", h2T3[:, fc, blk * 128:blk * 128 + 128], pv[:, fc * 128:(fc + 1) * 128],
                                   gmul.ap[:, 8 + fc:8 + fc + 1], modT.ap[:, 24 + fc:24 + fc + 1], ALU.mult, ALU.add,
                                   [PB[st_["bk"]], gmul.b, modT.b], [h2T.bufs[i]])
                    out += [f1, f2, f3]
                return out

            def pair_cols(p):
                if p < 4:
                    return p * 128, 512 + p * 128, 1024 + p * 128
                lp = p - 4
                return 1536 + lp * 128, 2048 + lp * 128, 2560 + lp * 128

            pair_state = {}

            def pair_setup(a):
                p = PAIR_ORDER[a]
                sl = a % 2
                fox = p >= 4
                kA = A.at(f"kA{a}", SLOT_OFF[sl], 4096, BF16, nbuf=8)
                kB = A.at(f"kB{a}", SLOT_OFF[sl] + 8 * KB, 4096, BF16, nbuf=8)
                vw = 130 if fox else 128
                V = A.at(f"V{a}", SLOT_OFF[sl] + 16 * KB, 32 * vw + 64, BF16, nbuf=8)
                MS("pool", V.ap[:, 32 * vw:32 * vw + 64], 0.0, V.bufs)
                wc = wcs[sl]
                wc4 = wc.ap.rearrange("p (w kc n) -> p w kc n", w=3, kc=8)
                cq, ck, cv = pair_cols(p)
                for wi, c0 in enumerate((cq, ck, cv)):
                    DMA("pool", wc4[:, wi, :, :], win_v[:, :, c0:c0 + 128], (), [wc.bufs[wi]], wc.bufs[wi], slow=True)
                if fox:
                    for kt_ in (kA, kB):
                        MS("pool", kt_.ap[64:128, :], 0.0, kt_.bufs)
                        MS("pool", kt_.ap[64:65, :], 1.0, kt_.bufs)
                        MS("pool", kt_.ap[96:97, :], 1.0, kt_.bufs)
                    V4 = V.ap[:, 0:32 * 130].rearrange("p (b h e) -> p b h e", b=32, h=2)
                    MS("pool", V4[:, :, :, 64:65], 1.0, V.bufs)
                pair_state[a] = dict(p=p, fox=fox, kA=kA, kB=kB, V=V, wc=wc, wc4=wc4, vw=vw)

            mb_ctr = [0]

            def qkv_minis(a, i, banks=(7,)):
                stt = pair_state[a]
                p, fox, kA, kB, V, wc, wc4 = stt["p"], stt["fox"], stt["kA"], stt["kB"], stt["V"], stt["wc"], stt["wc4"]
                qA, qB = qtiles[i % 2]
                tsl = slice(i * 512, (i + 1) * 512)
                out = []
                lp = p % 4
                cur = [banks[0]]

                def nb():
                    mb_ctr[0] += 1
                    cur[0] = banks[mb_ctr[0] % len(banks)]
                    return cur[0]

                def proj_stage(wi):
                    def f():
                        bk = nb()
                        for kc in range(8):
                            MM(psf(bk), wc4[:, wi, kc, :], h2T3[:, kc, tsl], kc == 0, kc == 7, [wc.bufs[wi], h2T.bufs[i]], [PB[bk]])
                    return f

                if not fox:
                    out.append(proj_stage(0))
                    out.append(lambda: TS("dve", qA.ap, psf(cur[0]), 0.125, None, ALU.mult, None, [PB[cur[0]]], [qA.b]))
                    out.append(proj_stage(1))
                    out.append(lambda: CP("dve", kA.ap[:, tsl], psf(cur[0]), [PB[cur[0]]], [kA.bufs[i]]))
                else:
                    def normed(wi, gcol, dstA, dstA_buf, dstB, dstB_buf, dcols):
                        def s1():
                            CP("dve", q32.ap, psf(cur[0]), [PB[cur[0]]], [q32.b])
                        def s2():
                            ACT(sqb.ap, q32.ap, AF.Square, [q32.b], [sqb.b])
                        def s3():
                            bk = nb()
                            MM(psf(bk), blk_bf, sqb.ap, True, True, [cbfA.b, sqb.b], [PB[bk]])
                        def s4():
                            ACT(rsq.ap, psf(cur[0]), AF.Ln, [PB[cur[0]], epsT.b], [rsq.b], bias=epsT.ap[:, 0:1])
                            ACT(rsq.ap, rsq.ap, AF.Exp, [rsq.b], [rsq.b], scale=-0.5)
                        def s5():
                            STT("dve", dstA[0:64, dcols], q32.ap[0:64, :], gqk.ap[0:64, gcol:gcol + 1], rsq.ap[0:64, :], ALU.mult, ALU.mult,
                                [q32.b, gqk.b, rsq.b], [dstA_buf])
                            STT("dve", dstB[0:64, dcols], q32.ap[64:128, :], gqk.ap[64:128, gcol:gcol + 1], rsq.ap[64:128, :], ALU.mult, ALU.mult,
                                [q32.b, gqk.b, rsq.b], [dstB_buf])
                        return [proj_stage(wi), s1, s2, s3, s4, s5]
                    out += normed(0, lp, qA.ap, qA.b, qB.ap, qB.b, slice(0, 512))
                    for hh, qx in enumerate((qA, qB)):
                        def r1(hh=hh):
                            h = 2 * lp + hh
                            bk = nb()
                            MM(psf(bk), sel_bf[0:40, h * 128:(h + 1) * 128], dF.ap[0:40, tsl], True, True, [cbf2.b, dF.bufs[i]], [PB[bk]])
                        def r2(qx=qx):
                            CP("dve", qx.ap[64:97, :], PS[cur[0]][64:97, :], [PB[cur[0]]], [qx.b])
                        out += [r1, r2]
                    out += normed(1, 4 + lp, kA.ap, kA.bufs[i], kB.ap, kB.bufs[i], tsl)

                def v1():
                    bk = nb()
                    for b in range(4):
                        for kc in range(8):
                            MM(PS[bk][:, b * 128:(b + 1) * 128], h2T3[:, kc, i * 512 + b * 128:i * 512 + (b + 1) * 128], wc4[:, 2, kc, :],
                               kc == 0, kc == 7, [wc.bufs[2], h2T.bufs[i]], [PB[bk]])
                def v2():
                    bk = cur[0]
                    if fox:
                        V4 = V.ap[:, 0:32 * 130].rearrange("p (b h e) -> p b h e", b=32, h=2)
                        CP("dve", V4[:, 4 * i:4 * i + 4, :, 0:64], psf(bk).rearrange("p (b h e) -> p b h e", b=4, h=2), [PB[bk]], [V.bufs[i]])
                    else:
                        V3 = V.ap[:, 0:32 * 128].rearrange("p (b n) -> p b n", b=32)
                        CP("dve", V3[:, 4 * i:4 * i + 4, :], psf(bk).rearrange("p (b n) -> p b n", b=4), [PB[bk]], [V.bufs[i]])
                out += [v1, v2]
                return out

            def fproj_mini(i):
                def f():
                    for b in range(4):
                        for kc in range(8):
                            MM(PS[7][:, b * 8:(b + 1) * 8], h2T3[:, kc, i * 512 + b * 128:i * 512 + (b + 1) * 128], wf3[:, kc, :],
                               kc == 0, kc == 7, [wf.b, h2T.bufs[i]], [PB[7]])
                    CP("dve", fT.ap[:, i * 32:(i + 1) * 32], PS[7][:, 0:32], [PB[7]], [fT.b])
                return f

            def lfc_minis():
                out = []
                def l1():
                    TT("dve", fT3, fT3, bfB.ap.rearrange("p (o h) -> p o h", o=1).to_broadcast([128, 32, 8]), ALU.add, [fT.b, bfB.b], [fT.b])
                    ACT(spT.ap, fT.ap, AF.Exp, [fT.b], [spT.b], scale=-1.0)
                    ACT(spT.ap, spT.ap, AF.Ln, [spT.b], [spT.b], bias=1.0)
                    MM(PS[7][:, 0:256], triL32, spT.ap, True, True, [cf_i.b, spT.b], [PB[7]])
                    MM(PS[7][:, 256:512], ones32.ap, spT.ap, True, True, [ones32.b, spT.b], [PB[7]])
                    CP("dve", lfcT.ap, PS[7][:, 0:256], [PB[7]], [lfcT.b])
                    CP("dve", totT.ap, PS[7][:, 256:512], [PB[7]], [totT.b])
                    MS("dve", offsT3[:, 0, :], 0.0, [offsT.b])
                    for b in range(1, 32):
                        TT("dve", offsT3[:, b, :], offsT3[:, b - 1, :], totT3[:, b - 1, :], ALU.subtract, [offsT.b, totT.b], [offsT.b])
                    TT("dve", lfcT.ap, lfcT.ap, offsT.ap, ALU.add, [lfcT.b, offsT.b], [lfcT.b])
                    for qt in range(8):
                        MM(PS[7][:, qt * 8:(qt + 1) * 8], ones32.ap[0:1, :], lfcT3[0:1, 4 * qt, :], True, True, [ones32.b, lfcT.b], [PB[7]])
                    CP("dve", crefB.ap, PS[7][:, 0:64], [PB[7]], [crefB.b])
                out.append(l1)
                for i in range(8):
                    def lr(i=i):
                        for b in range(4):
                            TR(PS[7][0:8, b * 128:(b + 1) * 128], lfcT3[:, 4 * i + b, :], ident32, [lfcT.b, cf_i.b], [PB[7]])
                        CP("dve", r32.ap[0:8, :], PS[7][0:8, :], [PB[7]], [r32.b])
                        TS("dve", d32.ap[0:8, :], r32.ap[0:8, :], r32.ap[0:8, 0:1], None, ALU.subtract, None, [r32.b], [d32.b])
                        CP("dve", hib.ap[0:8, :], d32.ap[0:8, :], [d32.b], [hib.b])
                        CP("dve", r32.ap[0:8, :], hib.ap[0:8, :], [hib.b], [r32.b])
                        TT("dve", dF.ap[32:40, i * 512:(i + 1) * 512], d32.ap[0:8, :], r32.ap[0:8, :], ALU.subtract, [d32.b, r32.b], [dF.bufs[i]])
                        CP("dve", dF.ap[0:8, i * 512:(i + 1) * 512], hib.ap[0:8, :], [hib.b], [dF.bufs[i]])
                    out.append(lr)
                return out

            def make_units(qt):
                us = []
                for hh in range(2):
                    nkb = 4 * qt + 4
                    for idx, kb in enumerate(range(nkb - 1, -1, -1)):
                        j = kb - 4 * qt
                        first = idx == 0
                        c0 = 0 if j < 0 else j * 128
                        us.append(dict(qt=qt, hh=hh, kb=kb, j=j, first=first, last=(kb == 0), c0=c0, ch=qt * 2 + hh))
                return us

            def run_pair(a, groups):
                stt = pair_state[a]
                p, fox, kA, kB, V = stt["p"], stt["fox"], stt["kA"], stt["kB"], stt["V"]
                units = []
                gstart = []
                for qt in range(8):
                    gstart.append(len(units))
                    units += make_units(qt)
                N = len(units)
                sched_minis = {}
                pre = {}
                for qt in range(8):
                    n0 = gstart[qt]
                    n1 = gstart[qt + 1] if qt < 7 else N
                    cnt = max(1, n1 - n0 - (2 if fox else 1))
                    ms = groups[qt]
                    for k_, m in enumerate(ms):
                        it = n0 + min(cnt - 1, (k_ * cnt) // max(1, len(ms)))
                        sched_minis.setdefault(it, []).append(m)
                if fox:
                    V4 = V.ap[:, 0:32 * 130].rearrange("p (b h e) -> p b h e", b=32, h=2)
                    lp = p - 4
                else:
                    V3 = V.ap[:, 0:32 * 128].rearrange("p (b n) -> p b n", b=32)
                atT_rows = [slice(0, 64), slice(64, 128)]

                FLAG_ = int(os.environ.get('FOX_LAG', '1'))

                def qk(n):
                    u = units[n]
                    a3 = n % 3
                    c0, kb, j, hh, qt = u["c0"], u["kb"], u["j"], u["hh"], u["qt"]
                    qA, qB = qtiles[qt % 2]
                    ks = slice(kb * 128, (kb + 1) * 128)
                    if fox:
                        qx = (qA, qB)[hh]
                        kx = (kA, kB)[hh]
                        MM(PS[a3][:, c0:512], kx.ap[0:97, ks], qx.ap[0:97, c0:512], True, j < 0, [kx.bufs[kb // 4], qx.b], [PB[a3]])
                        mk = mF_bf
                    else:
                        hr = atT_rows[hh]
                        MM(PS[a3][:, c0:512], kA.ap[hr, ks], qA.ap[hr, c0:512], True, False, [kA.bufs[kb // 4], qA.b], [PB[a3]], nochk=True)
                        mk = mS_bf
                    if j >= 0:
                        m0 = (3 - j) * 128 + c0
                        m1 = (3 - j) * 128 + (j + 1) * 128
                        MM(PS[a3][:, c0:(j + 1) * 128], ident_bf, mk[:, m0:m1], False, fox, [cbf.b, cbfA.b], [PB[a3]], nochk=not fox)

                ND_SB = int(os.environ.get("DUMMY_SB", "0"))
                ND_FOX = int(os.environ.get("DUMMY_FOX", "0"))
                DUMN = int(os.environ.get("DUMMY_N", "128"))
                DBANK = 4

                def dummies(k):
                    for _ in range(k):
                        S.op("pe", lambda e: e.matmul(PS[DBANK][:, 0:DUMN], lhsT=ident_bf, rhs=mS_bf[:, 0:DUMN], start=True, stop=True, skip_group_check=True), [], [])

                if not fox:
                    def it_body(n):
                        if n + 1 < N:
                            qk(n + 1)
                        dummies(ND_SB)
                        if n < N:
                            u = units[n]
                            a3, s2 = n % 3, n % 2
                            c0, ch = u["c0"], u["ch"] % 2
                            cs = slice(c0, 512)
                            e_ap = Esb.ap[:, s2 * 512:(s2 + 1) * 512]
                            sp_ap = SPsb.ap[:, s2 * 512:(s2 + 1) * 512]
                            tm_ap = tmps.ap[:, s2 * 512:(s2 + 1) * 512]
                            r_ap = Rsb.ap[:, ch * 512:(ch + 1) * 512]
                            ACT(e_ap[:, cs], PS[a3][:, cs], AF.Exp, [PB[a3]], [Esb.bufs[s2]])
                            ACT(sp_ap[:, cs], e_ap[:, cs], AF.Ln, [Esb.bufs[s2]], [SPsb.bufs[s2]], bias=1.0)
                            MM(PS[a3][:, cs], negU_bf, sp_ap[:, cs], False, True, [cbfA.b, SPsb.bufs[s2]], [PB[a3]], nochk=True)
                            if not u["last"]:
                                MM(PS[3][:, cs], negones.ap, sp_ap[:, cs], True, True, [negones.b, SPsb.bufs[s2]], [PB[3]])
                            if u["first"]:
                                if not u["last"]:
                                    MS("pool", r_ap[:, 0:c0], 0.0, [Rsb.bufs[ch]])
                                    CP("dve", r_ap[:, cs], PS[3][:, cs], [PB[3]], [Rsb.bufs[ch]])
                            else:
                                TT("dve", tm_ap[:, cs], PS[a3][:, cs], r_ap[:, cs], ALU.add, [PB[a3], Rsb.bufs[ch]], [tmps.bufs[s2]])
                                if not u["last"]:
                                    TT("dve", r_ap[:, cs], PS[3][:, cs], r_ap[:, cs], ALU.add, [PB[3], Rsb.bufs[ch]], [Rsb.bufs[ch]])
                        m = n - 1
                        if 0 <= m < N:
                            u = units[m]
                            a3, s2 = m % 3, m % 2
                            c0, ch = u["c0"], u["ch"] % 2
                            cs = slice(c0, 512)
                            w_ap = wsb.ap[:, s2 * 512:(s2 + 1) * 512]
                            if u["first"]:
                                ACT(w_ap[:, cs], PS[a3][:, cs], AF.Exp, [PB[a3]], [wsb.bufs[s2]])
                            else:
                                ACT(w_ap[:, cs], tmps.ap[:, s2 * 512:(s2 + 1) * 512][:, cs], AF.Exp, [tmps.bufs[s2]], [wsb.bufs[s2]])
                            hh, kb, qt = u["hh"], u["kb"], u["qt"]
                            vo = kb * 128 + hh * 64
                            if u["first"]:
                                MM(PS[5 + ch][:, 0:512], mS_bf[:, 512:640], mS_bf[:, 0:512], True, False, [cbfA.b], [PB[5 + ch]])
                            MM(PS[5 + ch][:, cs], V.ap[:, vo:vo + 128], w_ap[:, cs], False, u["last"],
                               [V.bufs[kb // 4], V.bufs[min(7, (kb + 1) // 4)], wsb.bufs[s2]], [PB[5 + ch]])
                            if u["last"]:
                                CP("dve", atT.ap[atT_rows[hh], qt * 512:(qt + 1) * 512], PS[5 + ch][0:64, :], [PB[5 + ch]], [atT.b])
                else:
                    def it_body(n):
                        if n + 2 < N:
                            qk(n + 2)
                        dummies(ND_FOX)
                        if n < N:
                            u = units[n]
                            a3 = n % 3
                            cs = slice(u["c0"], 512)
                            bq = biasq.ap[:, (u["qt"] % 2) * 256:(u["qt"] % 2 + 1) * 256].rearrange("p (b h) -> p b h", h=8)
                            ACT(Psb.ap[:, a3 * 512:(a3 + 1) * 512][:, cs], PS[a3][:, cs], AF.Exp, [PB[a3], biasq.bufs[u["qt"] % 2]], [Psb.bufs[a3]],
                                bias=bq[:, u["kb"], u["hh"]:u["hh"] + 1])
                        m = n - FLAG_
                        if 0 <= m < N:
                            u = units[m]
                            a3 = m % 3
                            cs = slice(u["c0"], 512)
                            ch = u["ch"] % 2
                            hh, kb, qt = u["hh"], u["kb"], u["qt"]
                            vo = kb * 130 + hh * 65
                            if u["first"]:
                                MM(PS[5 + ch][:, 0:512], mF_bf[:, 512:640], mF_bf[:, 0:512], True, False, [cbfA.b], [PB[5 + ch]])
                            MM(PS[5 + ch][:, cs], V.ap[:, vo:vo + 128], Psb.ap[:, a3 * 512:(a3 + 1) * 512][:, cs], False, u["last"],
                               [V.bufs[kb // 4], V.bufs[min(7, (kb + 1) // 4)], Psb.bufs[a3]], [PB[5 + ch]])
                            if u["last"]:
                                o_ap = osb.ap[:, ch * 512:(ch + 1) * 512]
                                ACT(o_ap[64:65, :], PS[5 + ch][64:65, :], AF.Ln, [PB[5 + ch]], [osb.bufs[ch]])
                                ACT(o_ap[64:65, :], o_ap[64:65, :], AF.Exp, [osb.bufs[ch]], [osb.bufs[ch]], scale=-1.0)
                        m = n - FLAG_ - 2
                        if 0 <= m < N and units[m]["last"]:
                            u = units[m]
                            ch = u["ch"] % 2
                            hh, qt = u["hh"], u["qt"]
                            o_ap = osb.ap[:, ch * 512:(ch + 1) * 512]
                            rb_ap = rbs.ap[:, ch * 512:(ch + 1) * 512]
                            MM(PS[3][0:64, :], ones32.ap[64:65, 0:64], o_ap[64:65, :], True, True, [ones32.b, osb.bufs[ch]], [PB[3]])
                            CP("dve", rb_ap[0:64, :], PS[3][0:64, :], [PB[3]], [rbs.bufs[ch]])
                            TT("dve", atT.ap[atT_rows[hh], qt * 512:(qt + 1) * 512], PS[5 + ch][0:64, :], rb_ap[0:64, :], ALU.mult,
                               [PB[5 + ch], rbs.bufs[ch]], [atT.b])

                def group_start(qt):
                    if fox:
                        lp_ = p - 4
                        nkb = 4 * qt + 4
                        bq = biasq.ap[:, (qt % 2) * 256:(qt % 2 + 1) * 256].rearrange("p (b h) -> p b h", h=8)
                        STT("dve", bq[:, 0:nkb, 0:2], lfcT3[:, 0:nkb, 2 * lp_:2 * lp_ + 2], -1.0,
                            crefB3[:, qt:qt + 1, 2 * lp_:2 * lp_ + 2].to_broadcast([128, nkb, 2]), ALU.mult, ALU.add,
                            [lfcT.b, crefB.b], [biasq.bufs[qt % 2]])

                gs = set(gstart)
                group_start(0)
                qk(0)
                if fox:
                    qk(1)
                for n in range(N + (5 if fox else 1)):
                    if (n + 1) in gs and n + 1 < N:
                        group_start(units[n + 1]["qt"])
                    it_body(n)
                    for m in sched_minis.get(n, []):
                        m()
                DMA("sp", AT_d[p], atT.ap, [atT.b], [atb[p]], atT.b)

            late = {}

            def prefetch_ffn2():
                W2 = ffn_weights("2", w2g_d, w2u_d, w2d_d, with_wd=False)
                for ld in W2["loads"][:8]:
                    ld()
                late["W2"] = W2

            def prep_wo():
                R3 = 132 * KB
                wo = A.at("wo", R3, 8 * DM, BF16)
                wo3 = wo.ap.rearrange("p (kc n) -> p kc n", kc=8)
                gb2 = A.at("gb2", R3 + 16 * KB, 1024, F32)
                dtmp2 = A.at("dtmp2", ATT_OFF + 8 * KB, 256, F32, nbuf=2)
                assert ATT_OFF + 9 * KB <= CA_OFF
                DMA("pool", wo3, wo_d.rearrange("(kc p) n -> p kc n", p=128), (), [wo.b], wo.b)
                gate_bcast_tile(1, gb2.ap, gb2.b, dtmp2, dtmp2.bufs, [7, 7])
                for kc in range(8):
                    TT("pool", wo3[:, kc, :], wo3[:, kc, :], gb2.ap, ALU.mult, [wo.b, gb2.b], [wo.b])
                late["wo"] = wo
                late["wo3"] = wo3

            if UPFRONT:
                def stA(kb_):
                    sl = kb_ % 4
                    xb = xn_b.bufs[sl]
                    xap = xn_b.ap[:, sl * 1024:(sl + 1) * 1024]
                    hap = hn_b.ap[:, sl * 1024:(sl + 1) * 1024]
                    sb_ = ss_b.bufs[sl]
                    sc_ap = ss_b.ap[:, sl * 4:sl * 4 + 1]
                    ln_ap = ss_b.ap[:, sl * 4 + 1:sl * 4 + 2]
                    rs_ap = ss_b.ap[:, sl * 4 + 2:sl * 4 + 3]
                    DMA("sp", xap, X1_d[kb_ * 128:(kb_ + 1) * 128, :], [x1b[kb_ // 4]], [xb], xb)
                    MS("dve", sc_ap, 0.0, [sb_])
                    ACT(hap, xap, AF.Square, [xb, sb_], [hn_b.bufs[sl], sb_], accum=sc_ap)
                    ACT(ln_ap, sc_ap, AF.Ln, [sb_], [sb_], bias=epsT.ap[:, 0:1], scale=1.0 / DM)
                    ACT(rs_ap, ln_ap, AF.Exp, [sb_], [sb_], scale=-0.5)

                def stB(kb_):
                    sl = kb_ % 4
                    TS("dve", hn_b.ap[:, sl * 1024:(sl + 1) * 1024], xn_b.ap[:, sl * 1024:(sl + 1) * 1024], ss_b.ap[:, sl * 4 + 2:sl * 4 + 3],
                       None, ALU.mult, None, [xn_b.bufs[sl], ss_b.bufs[sl]], [hn_b.bufs[sl]])

                def stC(kb_):
                    sl = kb_ % 4
                    bk = 4 + kb_ % 4
                    pv = psb16(bk)
                    hap = hn_b.ap[:, sl * 1024:(sl + 1) * 1024]
                    for fc in range(8):
                        TR(pv[:, fc * 128:(fc + 1) * 128], hap[:, fc * 128:(fc + 1) * 128], ident_bf, [hn_b.bufs[sl], cbf.b], [PB[bk]])

                def stD(kb_):
                    bk = 4 + kb_ % 4
                    pv = psb16(bk)
                    i = kb_ // 4
                    for fc in range(8):
                        if fc % 4 == 3:
                            ACT(h2T3[:, fc, kb_ * 128:kb_ * 128 + 128], pv[:, fc * 128:(fc + 1) * 128], AF.Identity,
                                [PB[bk], gmul.b, modT.b], [h2T.bufs[i]],
                                bias=modT.ap[:, 24 + fc:24 + fc + 1], scale=gmul.ap[:, 8 + fc:8 + fc + 1])
                        else:
                            TS("dve", h2T3[:, fc, kb_ * 128:kb_ * 128 + 128], pv[:, fc * 128:(fc + 1) * 128],
                               gmul.ap[:, 8 + fc:8 + fc + 1], modT.ap[:, 24 + fc:24 + fc + 1], ALU.mult, ALU.add,
                               [PB[bk], gmul.b, modT.b], [h2T.bufs[i]])

                for k_ in range(32 + 3):
                    if k_ < 32:
                        stA(k_)
                    if 0 <= k_ - 1 < 32:
                        stB(k_ - 1)
                    if 0 <= k_ - 2 < 32:
                        stC(k_ - 2)
                    if 0 <= k_ - 3 < 32:
                        stD(k_ - 3)
                for i in range(8):
                    fproj_mini(i)()
                pair_setup(0)
                for m in qkv_minis(0, 0):
                    m()
                for m in lfc_minis():
                    m()
            else:
                for m in norm2_minis(0):
                    m()
                pair_setup(0)
                for m in qkv_minis(0, 0):
                    m()
                fproj_mini(0)()
            npairs = int(os.environ.get("NPAIRS", "8")) if stage >= 3 else 1
            for a in range(npairs):
                groups = []
                for qt in range(8):
                    ms = []
                    cur_fox = PAIR_ORDER[a] >= 4
                    mbanks = (4, 7) if not (os.environ.get('DUMMY_FOX') or os.environ.get('DUMMY_SB')) else (7,)
                    if qt < 7:
                        if a == 0 and not UPFRONT:
                            ms += norm2_minis(qt + 1)
                        ms += qkv_minis(a, qt + 1, mbanks)
                        if a == 0 and not UPFRONT:
                            ms.append(fproj_mini(qt + 1))
                    else:
                        if a == 0 and not UPFRONT:
                            ms += lfc_minis()
                        if a == 7 and stage >= 4:
                            ms.append(prefetch_ffn2)
                        if a + 1 < npairs:
                            ms.append(lambda a=a: pair_setup(a + 1))
                            for k_ in range(20):
                                def qm(a=a, k_=k_, mbanks=mbanks):
                                    if ("q0", a) not in late:
                                        late[("q0", a)] = qkv_minis(a + 1, 0, mbanks)
                                    lst = late[("q0", a)]
                                    if k_ < len(lst):
                                        lst[k_]()
                                ms.append(qm)
                    if a == 5 and qt == 3 and stage >= 4:
                        ms.append(prep_wo)
                    groups.append(ms)
                run_pair(a, groups)
            if dbg and stage < 4:
                DMA("sp", H2_d, h2T.ap, h2T.bufs, [], h2T.bufs[0])
                final_bufs.append(h2T.bufs[0])
            final_bufs.append(atT.b)

        if stage >= 4:
            W2 = late["W2"]
            ffn_weights_wd(W2)
            wo, wo3 = late["wo"], late["wo3"]
            R3 = 132 * KB
            att_t = A.at("att_t", R3 + 16 * KB, 2 * 4096, BF16, nbuf=2)
            xr3 = A.at("xr3", R3 + 32 * KB, 2 * 4096, F32, nbuf=2)
            def p3_views(i):
                sl = i % 2
                at3 = att_t.ap[:, sl * 4096:(sl + 1) * 4096].rearrange("p (kc t) -> p kc t", kc=8)
                x3 = xr3.ap[:, sl * 4096:(sl + 1) * 4096].rearrange("p (b n) -> p b n", b=4)
                return sl, at3, x3

            def p3_load(i):
                sl, at3, x3 = p3_views(i)
                DMA("sp", at3, AT_d.rearrange("pr p t -> p pr t")[:, :, i * 512:(i + 1) * 512], atb, [att_t.bufs[sl]], att_t.bufs[sl])
                DMA("sp", x3, X1_d[i * 512:(i + 1) * 512, :].rearrange("(b p) n -> p b n", p=128), [x1b[i]], [xr3.bufs[sl]], xr3.bufs[sl])

            p3_load(0)
            for i in range(8):
                sl, at3, x3 = p3_views(i)
                if i + 1 < 8:
                    p3_load(i + 1)
                for b in range(4):
                    for nh in range(2):
                        bk = (b * 2 + nh) % 4
                        for kc in range(8):
                            MM(psf(bk), at3[:, kc, b * 128:(b + 1) * 128], wo3[:, kc, nh * 512:(nh + 1) * 512], kc == 0, kc == 7,
                               [att_t.bufs[sl], wo.b], [PB[bk]])
                        TT("dve", x3[:, b, nh * 512:(nh + 1) * 512], psf(bk), x3[:, b, nh * 512:(nh + 1) * 512], ALU.add,
                           [PB[bk], xr3.bufs[sl]], [xr3.bufs[sl]])
                DMA("sp", X1_d[i * 512:(i + 1) * 512, :].rearrange("(b p) n -> p b n", p=128), x3, [xr3.bufs[sl]], [x1b[i]], xr3.bufs[sl])
            for ld in W2["loads"][8:]:
                ld()
            gb3 = A.at("gb3", XRES_OFF, 1024, F32)
            dtmp3 = A.at("dtmp3", XRES_OFF + 4 * KB, 256, F32, nbuf=2)
            gate_bcast_tile(2, gb3.ap, gb3.b, dtmp3, dtmp3.bufs, [2, 3])
            fold_gate(W2, 2, gb3.ap, gb3.b)
            sg2 = A.at("sg2", SG_OFF, 1024, BF16, nbuf=2)
            if stage >= 5:
                xr = ffn_phase("2", X1_d, x1b, out_d, None, W2, 2, sg2)
                final_bufs = [xr.b]
            else:
                final_bufs += [xr3.bufs[0], xr3.bufs[1]]

        fin = S.op("sp", None, writes=final_bufs) if final_bufs else None
        S.emit()
    return nc


_CACHE = {}


def _inputs_for_core(inp, b, cbf, cf32):
    m = {
        "x": np.ascontiguousarray(inp["x"][b]),
        "c": np.ascontiguousarray(inp["c"][b:b + 1]),
        "w_mod": inp["w_mod"][0], "b_mod": inp["b_mod"][0:1],
        "g_ffn1": inp["g_ffn1"][0:1], "w1_gate": inp["w1_gate"][0], "w1_up": inp["w1_up"][0], "w1_down": inp["w1_down"][0],
        "g_mix": inp["g_mix"][0:1], "w_in": inp["w_in"][0], "b_f": inp["b_f"][0:1],
        "g_q": inp["g_q"][0].reshape(1, 512), "g_k": inp["g_k"][0].reshape(1, 512),
        "w_o": inp["w_o"][0], "g_ffn2": inp["g_ffn2"][0:1],
        "w2_gate": inp["w2_gate"][0], "w2_up": inp["w2_up"][0], "w2_down": inp["w2_down"][0],
        "cbf": cbf, "cf32": cf32,
    }
    return {k: np.ascontiguousarray(np.asarray(v, dtype=np.float32)) for k, v in m.items()}


def kernel(**inputs):
    cbf, cf32 = _consts()
    nc = build()
    in_maps = [_inputs_for_core(inputs, b, cbf, cf32) for b in range(8)]
    res = run_bass_kernel_spmd(nc, in_maps, core_ids=list(range(8)))
    return np.stack([np.asarray(r["out"], dtype=np.float32) for r in res.results], axis=0)
```
